# Optimizing a Trainium2 kernel written in Bass

```python
import jax, jax.numpy as jnp
from jax import lax
import numpy as np

D_MODEL = 1024
BATCH = 8
SEQ = 8192
DEPTH = 4

MEM_LEN = 256
MIX_WIDTH = 768
N_BRANCHES = 3
RNN_BLOCKS = 8
RNN_BLOCK = MIX_WIDTH // RNN_BLOCKS
RNN_CONV = 4
LRU_C = 8.0
GLA_HEADS = 4
GLA_DK = MIX_WIDTH // 2 // GLA_HEADS
GLA_DV = MIX_WIDTH // GLA_HEADS
GLA_LOW_RANK = 16
GLA_NORMALIZER = 16.0
GLA_CHUNK = 64
MEM_HEADS = 4
MEM_HEAD_DIM = MIX_WIDTH // MEM_HEADS
D_FF = 3 * D_MODEL
FFN_CONV = 3
EPS = 1e-6

IN_SPLITS = (MIX_WIDTH, GLA_HEADS * GLA_DK, GLA_HEADS * GLA_DK, GLA_HEADS * GLA_DV,
             GLA_HEADS * GLA_DV, GLA_LOW_RANK, MEM_HEADS * MEM_HEAD_DIM, N_BRANCHES * D_MODEL)
N_IN = sum(IN_SPLITS)

kernel_name = "hybrid_rglru_gla_memxattn_convffn"


def rms_norm(x, gain):
    x32 = x.astype(jnp.float32)
    y = x32 * lax.rsqrt(jnp.mean(x32 * x32, axis=-1, keepdims=True) + EPS)
    return (y * gain.astype(jnp.float32)).astype(x.dtype)


def causal_dwconv(x, w, b):
    width, ch = w.shape
    y = lax.conv_general_dilated(
        x, w[:, None, :].astype(x.dtype), window_strides=(1,), padding=[(width - 1, 0)],
        dimension_numbers=("NWC", "WIO", "NWC"), feature_group_count=ch)
    return y + b.astype(x.dtype)


def _linear_combine(left, right):
    a_l, b_l = left
    a_r, b_r = right
    return a_l * a_r, a_r * b_l + b_r


def rg_lru(xr, w_a, b_a, w_x, b_x, lam):
    bsz, slen, ch = xr.shape
    xb = xr.reshape(bsz, slen, RNN_BLOCKS, RNN_BLOCK)
    r = jax.nn.sigmoid(jnp.einsum("bshi,hij->bshj", xb, w_a) + b_a).reshape(bsz, slen, ch)
    i = jax.nn.sigmoid(jnp.einsum("bshi,hij->bshj", xb, w_x) + b_x).reshape(bsz, slen, ch)
    log_a = LRU_C * r.astype(jnp.float32) * jax.nn.log_sigmoid(lam.astype(jnp.float32))
    a = jnp.exp(log_a)
    u = jnp.sqrt(-jnp.expm1(2.0 * log_a)) * (i * xr).astype(jnp.float32)
    _, h = lax.associative_scan(_linear_combine, (a, u), axis=1)
    return h.astype(xr.dtype)


def chunked_gla(q, k, v, log_alpha):
    bsz, slen, nh, dk = q.shape
    dv = v.shape[-1]
    n_chunks = slen // GLA_CHUNK

    def to_chunks(t):
        return t.reshape(bsz, n_chunks, GLA_CHUNK, nh, t.shape[-1]).transpose(0, 1, 3, 2, 4)

    qc, kc, vc, gc = to_chunks(q), to_chunks(k), to_chunks(v), to_chunks(log_alpha)
    g_cum = jnp.cumsum(gc, axis=3)
    g_last = g_cum[:, :, :, -1:, :]
    q_e = qc * (dk ** -0.5) * jnp.exp(g_cum)
    k_e = kc * jnp.exp(-g_cum)
    k_end = kc * jnp.exp(g_last - g_cum)
    decay = jnp.exp(g_last[:, :, :, 0, :])
    mask = jnp.tril(jnp.ones((GLA_CHUNK, GLA_CHUNK), dtype=bool))
    scores = jnp.where(mask, jnp.einsum("bnhid,bnhjd->bnhij", q_e, k_e), 0.0)
    o_intra = jnp.einsum("bnhij,bnhjv->bnhiv", scores, vc)

    def step(state, xs):
        q_n, k_n, v_n, d_n = xs
        o_n = jnp.einsum("bhid,bhdv->bhiv", q_n, state)
        state = d_n[..., None] * state + jnp.einsum("bhjd,bhjv->bhdv", k_n, v_n)
        return state, o_n

    xs = (jnp.moveaxis(q_e, 1, 0), jnp.moveaxis(k_end, 1, 0), jnp.moveaxis(vc, 1, 0),
          jnp.moveaxis(decay, 1, 0))
    state0 = jnp.zeros((bsz, nh, dk, dv), jnp.float32)
    _, o_inter = lax.scan(step, state0, xs)
    o = o_intra + jnp.moveaxis(o_inter, 0, 1)
    return o.transpose(0, 1, 3, 2, 4).reshape(bsz, slen, nh, dv)


def setup_inputs(seed: int = 0) -> dict:
    key = jax.random.key(seed)
    ks = jax.random.split(key, 32)

    def nrm(k, shape, scale):
        return jax.random.normal(k, shape, jnp.float32) * scale

    def gain(k, shape):
        return 1.0 + 0.02 * jax.random.normal(k, shape, jnp.float32)

    u = jax.random.uniform(ks[10], (DEPTH, MIX_WIDTH), jnp.float32, minval=0.9, maxval=0.999)
    p = u ** (1.0 / LRU_C)
    rg_lambda = jnp.log(p) - jnp.log1p(-p)
    return {
        "x": nrm(ks[0], (BATCH, SEQ, D_MODEL), 1.0),
        "mem": nrm(ks[1], (BATCH, MEM_LEN, D_MODEL), 1.0),
        "norm_mix": gain(ks[2], (DEPTH, D_MODEL)),
        "w_in": nrm(ks[3], (DEPTH, D_MODEL, N_IN), D_MODEL ** -0.5),
        "rnn_conv_w": nrm(ks[4], (DEPTH, RNN_CONV, MIX_WIDTH), RNN_CONV ** -0.5),
        "rnn_conv_b": nrm(ks[5], (DEPTH, MIX_WIDTH), 0.01),
        "rg_w_a": nrm(ks[6], (DEPTH, RNN_BLOCKS, RNN_BLOCK, RNN_BLOCK), RNN_BLOCK ** -0.5),
        "rg_b_a": nrm(ks[7], (DEPTH, RNN_BLOCKS, RNN_BLOCK), 0.01),
        "rg_w_x": nrm(ks[8], (DEPTH, RNN_BLOCKS, RNN_BLOCK, RNN_BLOCK), RNN_BLOCK ** -0.5),
        "rg_b_x": nrm(ks[9], (DEPTH, RNN_BLOCKS, RNN_BLOCK), 0.01),
        "rg_lambda": rg_lambda,
        "gla_w_decay": nrm(ks[11], (DEPTH, GLA_LOW_RANK, GLA_HEADS * GLA_DK), GLA_LOW_RANK ** -0.5),
        "gla_b_decay": nrm(ks[12], (DEPTH, GLA_HEADS * GLA_DK), 0.01),
        "gla_norm": gain(ks[13], (DEPTH, GLA_HEADS * GLA_DV)),
        "mem_norm": gain(ks[14], (DEPTH, D_MODEL)),
        "w_mem_kv": nrm(ks[15], (DEPTH, D_MODEL, 2 * MEM_HEADS * MEM_HEAD_DIM), D_MODEL ** -0.5),
        "w_branch": nrm(ks[16], (DEPTH, N_BRANCHES, MIX_WIDTH, D_MODEL), MIX_WIDTH ** -0.5),
        "w_out": nrm(ks[17], (DEPTH, D_MODEL, D_MODEL), D_MODEL ** -0.5),
        "norm_ffn": gain(ks[18], (DEPTH, D_MODEL)),
        "w_up": nrm(ks[19], (DEPTH, D_MODEL, 2 * D_FF), D_MODEL ** -0.5),
        "ffn_conv_w": nrm(ks[20], (DEPTH, FFN_CONV, 2 * D_FF), FFN_CONV ** -0.5),
        "ffn_conv_b": nrm(ks[21], (DEPTH, 2 * D_FF), 0.01),
        "w_down": nrm(ks[22], (DEPTH, D_FF, D_MODEL), D_FF ** -0.5),
        "final_norm": gain(ks[23], (D_MODEL,)),
    }


def reference(x, mem, norm_mix, w_in, rnn_conv_w, rnn_conv_b, rg_w_a, rg_b_a, rg_w_x, rg_b_x,
              rg_lambda, gla_w_decay, gla_b_decay, gla_norm, mem_norm, w_mem_kv, w_branch, w_out,
              norm_ffn, w_up, ffn_conv_w, ffn_conv_b, w_down, final_norm):
    bsz, slen, _ = x.shape
    split_idx = np.cumsum(IN_SPLITS)[:-1].tolist()
    for l in range(DEPTH):
        h = rms_norm(x, norm_mix[l])
        z = h @ w_in[l]
        z_rnn, z_q, z_k, z_v, z_g, z_dec, z_mq, z_gate = jnp.split(z, split_idx, axis=-1)

        xr = causal_dwconv(z_rnn, rnn_conv_w[l], rnn_conv_b[l])
        out_a = rg_lru(xr, rg_w_a[l], rg_b_a[l], rg_w_x[l], rg_b_x[l], rg_lambda[l])

        dec_logit = (z_dec @ gla_w_decay[l] + gla_b_decay[l]).astype(jnp.float32)
        log_alpha = (jax.nn.log_sigmoid(dec_logit) / GLA_NORMALIZER).reshape(bsz, slen, GLA_HEADS, GLA_DK)
        o_gla = chunked_gla(
            z_q.astype(jnp.float32).reshape(bsz, slen, GLA_HEADS, GLA_DK),
            z_k.astype(jnp.float32).reshape(bsz, slen, GLA_HEADS, GLA_DK),
            z_v.astype(jnp.float32).reshape(bsz, slen, GLA_HEADS, GLA_DV),
            log_alpha).astype(x.dtype)
        o_gla = rms_norm(o_gla, gla_norm[l].reshape(GLA_HEADS, GLA_DV)).reshape(bsz, slen, -1)
        out_b = o_gla * jax.nn.silu(z_g)

        m = rms_norm(mem, mem_norm[l])
        mk, mv = jnp.split(m @ w_mem_kv[l], 2, axis=-1)
        mk = mk.reshape(bsz, MEM_LEN, MEM_HEADS, MEM_HEAD_DIM)
        mv = mv.reshape(bsz, MEM_LEN, MEM_HEADS, MEM_HEAD_DIM)
        mq = z_mq.reshape(bsz, slen, MEM_HEADS, MEM_HEAD_DIM)
        s = jnp.einsum("bshd,bmhd->bhsm", mq, mk).astype(jnp.float32) * (MEM_HEAD_DIM ** -0.5)
        prob = jax.nn.softmax(s, axis=-1).astype(mv.dtype)
        out_c = jnp.einsum("bhsm,bmhd->bshd", prob, mv).reshape(bsz, slen, MIX_WIDTH)

        branches = jnp.stack([out_a, out_b, out_c], axis=2)
        proj = jnp.einsum("bsnc,ncd->bsnd", branches, w_branch[l])
        gates = jax.nn.sigmoid(z_gate).reshape(bsz, slen, N_BRANCHES, D_MODEL)
        merged = jnp.sum(gates * proj, axis=2)
        x = x + merged @ w_out[l]

        h2 = rms_norm(x, norm_ffn[l])
        up = causal_dwconv(h2 @ w_up[l], ffn_conv_w[l], ffn_conv_b[l])
        gate_h, val_h = jnp.split(up, 2, axis=-1)
        x = x + (jax.nn.gelu(gate_h, approximate=True) * val_h) @ w_down[l]
    return rms_norm(x, final_norm)
```

```python
import numpy as np
import concourse.bass as bass
import concourse.mybir as mybir
from concourse.bass_utils import run_bass_kernel_spmd

F32 = mybir.dt.float32
BF16 = mybir.dt.bfloat16
AF = mybir.ActivationFunctionType
ALU = mybir.AluOpType

D = 1024
NL = 4
SEQ = 8192
NB = 8
MEM = 256
MIX = 768
T = 512
EPS = 1e-6
NIN = 6928
SLOTW = 4096
C_GLA = 128

V128_L = 222
O_NMIX, O_NFFN, O_NMEM, O_GLAN, O_FW0, O_FW1, O_FW2, O_FB = 0, 8, 16, 24, 30, 78, 126, 174
O_FINAL = 4 * V128_L
R128 = O_FINAL + 8
V96_L = 68
O_CW0, O_CB, O_BA, O_BX, O_LAM, O_BD = 0, 32, 40, 48, 56, 64
R96 = 4 * V96_L


class Tile:
    __slots__ = ("t", "w", "r", "dsem", "dcnt", "name")

    def __init__(self, t, name=""):
        self.t = t
        self.w = None
        self.r = {}
        self.dsem = None
        self.dcnt = 0
        self.name = name


class Sched:
    def __init__(self, nc):
        self.nc = nc
        self.engs = {"pe": nc.tensor, "dve": nc.vector, "act": nc.scalar,
                     "pool": nc.gpsimd, "sp": nc.sync}
        self.sem = {k: nc.alloc_semaphore("sem_" + k) for k in self.engs}
        self.cnt = {k: 0 for k in self.engs}
        self.known = {k: {} for k in self.engs}
        self.semobj = {}
        self.nwait = 0
        self.ninst = 0
        self.out_waits = []
        self._uid = 0
        self.recording = True
        self.rec = []
        self._grp = []
        self._grp_dur = 0.0
        self.sim_time = 0.0
        for k, s in self.sem.items():
            self.semobj[id(s)] = s

    def uid(self):
        self._uid += 1
        return self._uid

    def _wait(self, E, deps):
        eng = self.engs[E]
        kn = self.known[E]
        best = {}
        for (s, v) in deps:
            k = id(s)
            if kn.get(k, 0) >= v:
                continue
            if best.get(k, (None, 0))[1] < v:
                best[k] = (s, v)
        for k, (s, v) in best.items():
            if s is self.sem["pe"]:
                assert v <= self.cnt["pe"], "wait on an un-signalled PE group (deadlock)"
            eng.wait_ge(s, v)
            kn[k] = v
            self.nwait += 1

    def _deps(self, E, R, W):
        own = self.sem[E]
        deps = []
        for t in R:
            if t.w is not None:
                if not (E == "pe" and t.w[0] is own):
                    deps.append(t.w)
        for t in W:
            if t.w is not None:
                if not (E == "pe" and t.w[0] is own):
                    deps.append(t.w)
            for k, (s, v) in t.r.items():
                if s is own:
                    continue
                deps.append((s, v))
        return deps

    def op(self, E, fn, R=(), W=(), sig=True, dur=0.5):
        if not self.recording:
            return self._op_now(E, fn, R, W, sig)
        if E == "pe":
            self._grp.append((fn, list(R), list(W)))
            self._grp_dur += dur
            if sig:
                Rs, Ws = [], []
                for (_, r, w) in self._grp:
                    Rs += r
                    Ws += w
                self.rec.append(("pe", self._grp, Rs, Ws, self._grp_dur))
                self._grp = []
                self._grp_dur = 0.0
        else:
            self.rec.append((E, [(fn, list(R), list(W))], list(R), list(W), dur))

    def dma(self, E, out_ap, in_ap, R=(), W=(), dur=6.0):
        if not self.recording:
            return self._dma_now(E, out_ap, in_ap, R, W)
        self.rec.append(("dma:" + E, (out_ap, in_ap), list(R), list(W), dur))

    def schedule_and_emit(self, reorder=True):
        import heapq
        rec = self.rec
        n = len(rec)
        assert not self._grp
        succ = [[] for _ in range(n)]
        indeg = [0] * n
        lastw = {}
        readers = {}
        for i, (E, body, R, W, dur) in enumerate(rec):
            preds = set()
            for t in R:
                k = id(t)
                if k in lastw:
                    preds.add(lastw[k])
            for t in W:
                k = id(t)
                if k in lastw:
                    preds.add(lastw[k])
                for r in readers.get(k, ()):
                    preds.add(r)
            preds.discard(i)
            for p in preds:
                succ[p].append(i)
            indeg[i] = len(preds)
            for t in W:
                k = id(t)
                lastw[k] = i
                readers[k] = []
            for t in R:
                k = id(t)
                if lastw.get(k) != i:
                    readers.setdefault(k, []).append(i)
        order = []
        if not reorder:
            order = list(range(n))
        else:
            engs = ["pe", "dve", "act", "pool", "sp"]
            efree = {e: 0.0 for e in engs}
            fut = {e: [] for e in engs}
            now = {e: [] for e in engs}
            rtime = [0.0] * n

            def eng_of(i):
                E = rec[i][0]
                return E[4:] if E.startswith("dma:") else E
            for i in range(n):
                if indeg[i] == 0:
                    heapq.heappush(fut[eng_of(i)], (0.0, i))
            done = 0
            while done < n:
                best = None
                for e in engs:
                    f, nw = fut[e], now[e]
                    while f and f[0][0] <= efree[e]:
                        heapq.heappush(nw, heapq.heappop(f)[1])
                    if nw:
                        cand = (efree[e], nw[0], e, True)
                    elif f:
                        cand = (f[0][0], f[0][1], e, False)
                    else:
                        continue
                    if best is None or cand[:2] < best[:2]:
                        best = cand
                st, i, e, from_now = best
                if from_now:
                    heapq.heappop(now[e])
                else:
                    heapq.heappop(fut[e])
                E, body, R, W, dur = rec[i]
                if E.startswith("dma:"):
                    efree[e] = st + 0.15
                    fin = st + dur
                else:
                    efree[e] = st + dur
                    fin = st + dur
                order.append(i)
                done += 1
                for j in succ[i]:
                    indeg[j] -= 1
                    if rtime[j] < fin + 0.1:
                        rtime[j] = fin + 0.1
                    if indeg[j] == 0:
                        heapq.heappush(fut[eng_of(j)], (rtime[j], j))
            self.sim_time = max(efree.values())
        self.recording = False
        for i in order:
            E, body, R, W, dur = rec[i]
            if E.startswith("dma:"):
                self._dma_now(E[4:], body[0], body[1], R, W)
            else:
                m = len(body)
                for k, (fn, r, w) in enumerate(body):
                    self._op_now(E, fn, r, w, (k == m - 1) if E == "pe" else True)

    def _op_now(self, E, fn, R=(), W=(), sig=True):
        self._wait(E, self._deps(E, R, W))
        ins = fn(self.engs[E])
        self.ninst += 1
        s = self.sem[E]
        if sig:
            self.cnt[E] += 1
            ins.then_inc(s, 1)
            v = self.cnt[E]
        else:
            v = self.cnt[E] + 1
        for t in R:
            if t.r.get(id(s), (s, 0))[1] < v:
                t.r[id(s)] = (s, v)
        for t in W:
            t.w = (s, v)
            t.r = {}
        return ins

    def _dma_now(self, E, out_ap, in_ap, R=(), W=()):
        self._wait(E, self._deps(E, R, W))
        ins = self.engs[E].dma_start(out=out_ap, in_=in_ap)
        self.ninst += 1
        for t in list(W) + list(R):
            if t.dsem is None:
                t.dsem = self.nc.alloc_semaphore("dsem%d" % self.uid())
        tt = W[0] if len(W) else R[0]
        tt.dcnt += 16
        ins.then_inc(tt.dsem, 16)
        for t in W:
            t.w = (tt.dsem, tt.dcnt)
            t.r = {}
        for t in R:
            t.r[id(tt.dsem)] = (tt.dsem, tt.dcnt)
        return tt


def _colslab(W, cols):
    n = len(cols)
    sub = W[:, cols]
    return np.ascontiguousarray(sub.reshape(8, 128, n).transpose(1, 0, 2).reshape(128, 8 * n))


def pack_weights(inp, L):
    w_in = inp["w_in"]
    win_s = np.zeros((L, 18, 128, 3072), np.float32)
    wdec_s = np.zeros((L, 128, 128), np.float32)
    wmkv_s = np.zeros((L, 4, 128, 3072), np.float32)
    wout_s = np.zeros((L, 4, 128, 2048), np.float32)
    wup_s = np.zeros((L, 12, 128, 4096), np.float32)
    wdn_s = np.zeros((L, 8, 128, 3072), np.float32)
    wbr_s = np.zeros((L, 8, 128, 2816), np.float32)
    wrg_s = np.zeros((L, 96, 1536), np.float32)
    wdcy_s = np.zeros((L, 16, 384), np.float32)
    ar = np.arange
    for l in range(L):
        W = w_in[l]
        groups = []
        for s in range(2):
            groups.append(ar(0, 384) + s * 384)
        groups.append(ar(768, 1152))
        groups.append(ar(1152, 1536))
        for s in range(2):
            groups.append(ar(1536, 1920) + s * 384)
        for s in range(2):
            groups.append(ar(2304, 2688) + s * 384)
        for s in range(2):
            groups.append(ar(3088, 3472) + s * 384)
        for dc in range(8):
            groups.append(np.concatenate([3856 + b * 1024 + dc * 128 + ar(128) for b in range(3)]))
        assert len(groups) == 18
        for i, g in enumerate(groups):
            win_s[l, i] = _colslab(W, g)
        wdec_s[l] = _colslab(W, ar(3072, 3088))
        for i in range(4):
            wmkv_s[l, i] = _colslab(inp["w_mem_kv"][l], ar(384) + i * 384)
        for i in range(4):
            wout_s[l, i] = _colslab(inp["w_out"][l], ar(256) + i * 256)
        for j in range(12):
            cols = np.concatenate([np.concatenate([(2 * j + pi) * 128 + ar(128),
                                                   3072 + (2 * j + pi) * 128 + ar(128)]) for pi in range(2)])
            wup_s[l, j] = _colslab(inp["w_up"][l], cols)
        wd = inp["w_down"][l]
        for dc in range(8):
            sub = wd[:, dc * 128:(dc + 1) * 128]
            wdn_s[l, dc] = sub.reshape(24, 128, 128).transpose(1, 0, 2).reshape(128, 3072)
        wb = inp["w_branch"][l]
        for dc in range(8):
            blk = np.zeros((128, 22, 128), np.float32)
            a = wb[0][:, dc * 128:(dc + 1) * 128].reshape(8, 96, 128)
            blk[:96, 0:8] = a.transpose(1, 0, 2)
            b = wb[1][:, dc * 128:(dc + 1) * 128].reshape(6, 128, 128)
            blk[:, 8:14] = b.transpose(1, 0, 2)
            c = wb[2][:, dc * 128:(dc + 1) * 128].reshape(8, 96, 128)
            blk[:96, 14:22] = c.transpose(1, 0, 2)
            wbr_s[l, dc] = blk.reshape(128, 2816)
        rg = np.stack([inp["rg_w_a"][l], inp["rg_w_x"][l]], 0)
        wrg_s[l] = rg.transpose(2, 0, 1, 3).reshape(96, 1536)
        wdcy_s[l] = inp["gla_w_decay"][l]
    return dict(win_s=win_s, wdec_s=wdec_s, wmkv_s=wmkv_s, wout_s=wout_s, wup_s=wup_s,
                wdn_s=wdn_s, wbr_s=wbr_s, wrg_s=wrg_s, wdcy_s=wdcy_s)


def pack_small(inp, L):
    p128 = np.zeros((R128, 128), np.float32)
    p96 = np.zeros((R96, 96), np.float32)
    for l in range(L):
        b = l * V128_L
        p128[b + O_NMIX:b + O_NMIX + 8] = inp["norm_mix"][l].reshape(8, 128)
        p128[b + O_NFFN:b + O_NFFN + 8] = inp["norm_ffn"][l].reshape(8, 128)
        p128[b + O_NMEM:b + O_NMEM + 8] = inp["mem_norm"][l].reshape(8, 128)
        p128[b + O_GLAN:b + O_GLAN + 6] = inp["gla_norm"][l].reshape(6, 128)
        for k in range(3):
            o = (O_FW0, O_FW1, O_FW2)[k]
            p128[b + o:b + o + 48] = inp["ffn_conv_w"][l, k].reshape(48, 128)
        p128[b + O_FB:b + O_FB + 48] = inp["ffn_conv_b"][l].reshape(48, 128)
        c = l * V96_L
        for k in range(4):
            p96[c + O_CW0 + 8 * k:c + O_CW0 + 8 * k + 8] = inp["rnn_conv_w"][l, k].reshape(8, 96)
        p96[c + O_CB:c + O_CB + 8] = inp["rnn_conv_b"][l].reshape(8, 96)
        p96[c + O_BA:c + O_BA + 8] = inp["rg_b_a"][l].reshape(8, 96)
        p96[c + O_BX:c + O_BX + 8] = inp["rg_b_x"][l].reshape(8, 96)
        p96[c + O_LAM:c + O_LAM + 8] = inp["rg_lambda"][l].reshape(8, 96)
        p96[c + O_BD:c + O_BD + 4] = inp["gla_b_decay"][l].reshape(4, 96)
    p128[O_FINAL:O_FINAL + 8] = inp["final_norm"].reshape(8, 128)
    return p128, p96


def make_consts():
    ident = np.eye(128, dtype=np.float32)
    tri = np.triu(np.ones((128, 128), np.float32))
    tri4 = np.tile(tri, (1, 4))
    smask = np.ones((128, T), np.float32)
    smask[:, ::C_GLA] = 0.0
    ones = np.ones((128, 128), np.float32)
    epsc = np.full((128, 8), EPS, np.float32)
    return np.ascontiguousarray(np.concatenate([ident, ones, tri4, smask, epsc], axis=1))


def build(L=NL, S_TOK=SEQ, dbg=False):
    import os
    STAGE = int(os.environ.get("KSTAGE", "9"))
    NT = S_TOK // T
    nc = bass.Bass("TRN2", target_bir_lowering=False)
    S = Sched(nc)

    def dram(name, shape, kind="ExternalInput", dt=F32):
        return nc.dram_tensor(name, list(shape), dt, kind=kind).ap()

    x_d = dram("x", [S_TOK, D])
    mem_d = dram("mem", [MEM, D])
    p128_d = dram("p128", [R128, 128])
    p96_d = dram("p96", [R96, 96])
    cst_d = dram("cst", [128, 1288])
    win_d = dram("win_s", [L, 18, 128, 3072])
    wdec_d = dram("wdec_s", [L, 128, 128])
    wmkv_d = dram("wmkv_s", [L, 4, 128, 3072])
    wout_d = dram("wout_s", [L, 4, 128, 2048])
    wup_d = dram("wup_s", [L, 12, 128, 4096])
    wdn_d = dram("wdn_s", [L, 8, 128, 3072])
    wbr_d = dram("wbr_s", [L, 8, 128, 2816])
    wrg_d = dram("wrg_s", [L, 96, 1536])
    wdcy_d = dram("wdcy_s", [L, 16, 384])
    y_d = dram("y", [S_TOK, D], kind="ExternalOutput")

    def sb(name, shape, dt=F32):
        return nc.alloc_sbuf_tensor(name, list(shape), dt)

    cstF = Tile(sb("cstF", [128, 1288]), "cstF")
    identF = cstF.t[:, 0:128]
    onesF = cstF.t[:, 128:256]
    smask = cstF.t[:, 768:1280]
    epsc = cstF.t[:, 1280:1281]
    cstB = Tile(sb("cstB", [128, 768], BF16), "cstB")
    identB = cstB.t[:, 0:128]
    onesB = cstB.t[:, 128:256]
    tri4B = cstB.t[:, 256:768]

    V128 = Tile(sb("V128", [128, R128]), "V128")
    H128 = Tile(sb("H128", [128, R128]), "H128")
    V96 = Tile(sb("V96", [96, R96]), "V96")
    H96 = Tile(sb("H96", [96, R96]), "H96")
    N96 = Tile(sb("N96", [96, R96]), "N96")
    CL96 = Tile(sb("CL96", [96, R96]), "CL96")
    HCL96 = Tile(sb("HCL96", [96, R96]), "HCL96")
    wdcyB = Tile(sb("wdcyB", [16, L * 384], BF16), "wdcyB")

    Xall = sb("Xall", [128, 8, T])
    X = [Tile(Xall[:, c, :], "X%d" % c) for c in range(8)]
    Hball = sb("Hball", [128, 8, T], BF16)
    Hb = [Tile(Hball[:, c, :], "Hb%d" % c) for c in range(8)]
    MKT_one = Tile(sb("MKT", [96, 8 * MEM], BF16), "MKT")
    MV_one = Tile(sb("MV", [128, 2 * MIX], BF16), "MV")
    MKT = [MKT_one for l in range(L)]
    MV = [MV_one for l in range(L)]
    mkt_scr = [Tile(nc.dram_tensor("mkt_scr%d" % l, [96, 8 * MEM], BF16, kind="Internal").ap(), "mkts") for l in range(L)]
    mv_scr = [Tile(nc.dram_tensor("mv_scr%d" % l, [128, 2 * MIX], BF16, kind="Internal").ap(), "mvs") for l in range(L)]
    Sst_all = sb("Sst", [96, L, MIX])
    Sst = [[Tile(Sst_all[:, l, hh * 192:(hh + 1) * 192], "S") for hh in range(4)] for l in range(L)]
    Sb_one = Tile(sb("Sb", [96, MIX], BF16), "Sb")
    Sb_t = [Sb_one for l in range(L)]
    rhalo_all = sb("rhalo", [96, L, 8, 3])
    rhalo = [[Tile(rhalo_all[:, l, h, :], "rh") for h in range(8)] for l in range(L)]
    hst_all = sb("hst", [96, L, 8])
    hst = [[Tile(hst_all[:, l, h:h + 1], "hst") for h in range(8)] for l in range(L)]
    fh_all = sb("fh", [128, L, 48, 2])
    fh = [[Tile(fh_all[:, l, c, :], "fh") for c in range(48)] for l in range(L)]
    outB3a = sb("outB3", [128, 6, T], BF16)
    outB3 = Tile(outB3a, "outB3")
    sgg3a = sb("sgg3", [128, 6, T], BF16)
    sgg3 = Tile(sgg3a, "sgg3")
    zbuf = [Tile(sb("zb%d" % i, [96, T + 3]), "zb") for i in range(2)]
    xio = [Tile(sb("xio%d" % i, [128, D]), "xio") for i in range(2)]
    wslots = [Tile(sb("wslot%d" % i, [128, SLOTW], BF16), "wslot") for i in range(4)]
    wrgT = [Tile(sb("wrgT%d" % i, [96, 1536], BF16), "wrgT") for i in range(2)]
    vt = [Tile(sb("vt%d" % s, [128, MIX], BF16), "vt") for s in range(4)]
    ktt = [Tile(sb("ktt%d" % i, [128, 384], BF16), "ktt") for i in range(2)]
    scT = [Tile(sb("scT%d" % i, [128, 512], BF16), "scT") for i in range(2)]
    onb = [Tile(sb("on%d" % i, [128, MIX], BF16), "on") for i in range(2)]
    ssq = [Tile(sb("ssq%d" % i, [128, 8]), "ssq") for i in range(2)]
    junk = Tile(sb("junk", [128, 192]), "junk")
    zd = Tile(sb("zd", [16, T], BF16), "zd")
    dcy = [Tile(sb("dcy%d" % hh, [96, 4]), "dcy") for hh in range(4)]

    NF, NBF = 15, 32
    fpool = [Tile(sb("fp%d" % i, [128, T]), "fp%d" % i) for i in range(NF)]
    bpool = [Tile(sb("bp%d" % i, [128, T], BF16), "bp%d" % i) for i in range(NBF)]
    ffree = list(fpool)
    bfree = list(bpool)

    def allocF():
        return ffree.pop(0)

    def freeF(t):
        ffree.append(t)

    def allocB():
        return bfree.pop(0)

    def freeB(t):
        bfree.append(t)

    banks = [Tile(nc.alloc_psum_tensor("psb%d" % i, [128, 512], F32), "ps%d" % i) for i in range(8)]
    bank_i = [0]

    def PS():
        b = banks[bank_i[0] % 8]
        bank_i[0] += 1
        return b

    def layer_slabs(l):
        o = [("winA0", win_d[l, 0, :, :], 128, 3072),
             ("winA1", win_d[l, 1, :, :], 128, 3072), ("wdec", wdec_d[l, :, :], 128, 128)]
        for i in range(2, 10):
            o.append(("win%d" % i, win_d[l, i, :, :], 128, 3072))
        for dc in range(8):
            o.append(("wg%d" % dc, win_d[l, 10 + dc, :, :], 128, 3072))
            o.append(("wb%d" % dc, wbr_d[l, dc, :, :], 128, 2816))
        for i in range(4):
            o.append(("wout%d" % i, wout_d[l, i, :, :], 128, 2048))
        for j in range(12):
            o.append(("wup%d" % j, wup_d[l, j, :, :], 128, 4096))
        for dc in range(8):
            o.append(("wdn%d" % dc, wdn_d[l, dc, :, :], 128, 3072))
        return [(k + "_%d" % l, ap, P, F) for (k, ap, P, F) in o]

    SLABS = []
    if STAGE >= 3:
        for l in range(L):
            for i in range(4):
                SLABS.append(("wmkv%d_%d" % (i, l), wmkv_d[l, i, :, :], 128, 3072))
    if STAGE >= 4:
        for it in range(NT):
            for l in range(L):
                SLABS.extend(layer_slabs(l))
    NSLOT, AHEAD = 4, 2
    slab_pos = [0, 0]

    def load_slab(key):
        idx = slab_pos[0]
        slab_pos[0] += 1
        assert SLABS[idx][0] == key, (SLABS[idx][0], key)
        while slab_pos[1] <= min(idx + AHEAD, len(SLABS) - 1):
            j = slab_pos[1]
            _, ap, P, F = SLABS[j]
            t = wslots[j % NSLOT]
            S.dma("pool", t.t[:P, :F], ap, W=[t])
            slab_pos[1] += 1
        return wslots[idx % NSLOT]

    def fsz(ap):
        n = 1
        for d in ap.shape[1:]:
            n *= d
        return n

    def mm(ps, out_ap, lhsT, rhs, start, stop, R, sig=None):
        d = max(0.065, fsz(out_ap) / 2400.0 + 0.012)
        if lhsT.dtype == F32:
            d *= 4.0
        S.op("pe", lambda e: e.matmul(out_ap, lhsT, rhs, start=start, stop=stop), R=R, W=[ps],
             sig=(stop if sig is None else sig), dur=d)

    def tr(ps, out_ap, in_ap, ident, R, sig=True):
        S.op("pe", lambda e: e.transpose(out_ap, in_ap, ident), R=R, W=[ps], sig=sig,
             dur=(0.3 if in_ap.dtype == F32 else 0.1))

    def act(out, in_, func, R, W, bias=0.0, scale=1.0, accum_out=None):
        d = 0.3 + fsz(out) / 1200.0
        if accum_out is None:
            S.op("act", lambda e: e.activation(out=out, in_=in_, func=func, bias=bias, scale=scale), R=R, W=W, dur=d)
        else:
            S.op("act", lambda e: e.activation(out=out, in_=in_, func=func, bias=bias, scale=scale,
                                               accum_out=accum_out), R=R, W=W, dur=d)

    def edur(E, out, f):
        n = fsz(out)
        if E == "pool":
            return 0.15 + n * 0.0016
        return 0.12 + n * 0.00125 * f

    def tt(E, out, in0, in1, op, R, W):
        S.op(E, lambda e: e.tensor_tensor(out=out, in0=in0, in1=in1, op=op), R=R, W=W, dur=edur(E, out, 1.0))

    def ts(E, out, in0, s1, s2, op0, op1, R, W):
        if s2 is None:
            S.op(E, lambda e: e.tensor_scalar(out=out, in0=in0, scalar1=s1, scalar2=None, op0=op0), R=R, W=W,
                 dur=edur(E, out, 0.7))
        else:
            S.op(E, lambda e: e.tensor_scalar(out=out, in0=in0, scalar1=s1, scalar2=s2, op0=op0, op1=op1),
                 R=R, W=W, dur=edur(E, out, 0.7))

    def stt(E, out, in0, scalar, in1, op0, op1, R, W):
        S.op(E, lambda e: e.scalar_tensor_tensor(out=out, in0=in0, scalar=scalar, in1=in1, op0=op0, op1=op1),
             R=R, W=W, dur=edur(E, out, 1.0))

    def cp(E, out, in_, R, W):
        if E == "act":
            S.op("act", lambda e: e.activation(out=out, in_=in_, func=AF.Copy), R=R, W=W, dur=0.3 + fsz(out) / 1200.0)
        else:
            S.op(E, lambda e: e.tensor_copy(out=out, in_=in_), R=R, W=W, dur=edur(E, out, 0.7))

    def proj_fm(slot, ncol, col0, M, ps, rhs_tiles, n=T):
        for kc in range(8):
            mm(ps, ps.t[:M, :n], slot.t[:, kc * ncol + col0:kc * ncol + col0 + M], rhs_tiles[kc].t[:, :n],
               kc == 0, kc == 7, R=[slot, rhs_tiles[kc]])

    def rms_to_bf(Xc, gcol, dst, n=T):
        ps = PS()
        for c in range(8):
            sq = allocF()
            act(sq.t[:, :n], Xc[c].t[:, :n], AF.Square, R=[Xc[c]], W=[sq])
            mm(ps, ps.t[:, :n], onesF, sq.t[:, :n], c == 0, c == 7, R=[cstF, sq], sig=True)
            freeF(sq)
        rstd = allocF()
        act(rstd.t[:, :n], ps.t[:, :n], AF.Ln, R=[ps, cstF], W=[rstd], scale=1.0 / D, bias=epsc)
        act(rstd.t[:, :n], rstd.t[:, :n], AF.Exp, R=[rstd], W=[rstd], scale=-0.5)
        for c in range(8):
            stt("dve", dst[c].t[:, :n], Xc[c].t[:, :n], V128.t[:, gcol + c:gcol + c + 1], rstd.t[:, :n],
                ALU.mult, ALU.mult, R=[Xc[c], V128, rstd], W=[dst[c]])
        freeF(rstd)

    S.dma("sp", cstF.t[:, :], cst_d[:, :], W=[cstF])
    cp("dve", cstB.t[:, 0:768], cstF.t[:, 0:768], R=[cstF], W=[cstB])

    def load_T(dst, dram, rows, width):
        r0 = 0
        while r0 < rows:
            n = min(128, rows - r0)
            st = allocF()
            S.dma("sp", st.t[:n, :width], dram[r0:r0 + n, :], W=[st])
            ps = PS()
            tr(ps, ps.t[:width, :n], st.t[:n, :width], identF[:n, :n], R=[st, cstF])
            cp("dve", dst.t[:width, r0:r0 + n], ps.t[:width, :n], R=[ps], W=[dst])
            freeF(st)
            r0 += n

    load_T(V128, p128_d, R128, 128)
    load_T(V96, p96_d, R96, 96)
    ts("dve", H128.t[:, :], V128.t[:, :], 0.5, None, ALU.mult, None, R=[V128], W=[H128])
    ts("dve", H96.t[:, :], V96.t[:, :], 0.5, None, ALU.mult, None, R=[V96], W=[H96])
    ts("dve", N96.t[:, :], V96.t[:, :], -1.0, None, ALU.mult, None, R=[V96], W=[N96])
    act(CL96.t[:, :], V96.t[:, :], AF.Exp, R=[V96], W=[CL96], scale=-1.0)
    act(CL96.t[:, :], CL96.t[:, :], AF.Ln, R=[CL96], W=[CL96], bias=1.0)
    ts("dve", HCL96.t[:, :], CL96.t[:, :], -4.0, None, ALU.mult, None, R=[CL96], W=[HCL96])
    ts("dve", CL96.t[:, :], CL96.t[:, :], -8.0, None, ALU.mult, None, R=[CL96], W=[CL96])
    for l in range(L if STAGE >= 2 else 0):
        S.dma("pool", wdcyB.t[:16, l * 384:(l + 1) * 384], wdcy_d[l, :, :], W=[wdcyB])

    S.op("pool", lambda e: e.memset(Sst_all[:, :, :], 0.0), W=[t for row in Sst for t in row])
    S.op("pool", lambda e: e.memset(rhalo_all[:, :, :, :], 0.0), W=[t for row in rhalo for t in row])
    S.op("pool", lambda e: e.memset(hst_all[:, :, :], 0.0), W=[t for row in hst for t in row])
    S.op("pool", lambda e: e.memset(fh_all[:, :, :, :], 0.0), W=[t for row in fh for t in row])

    memT = [allocF() for _ in range(8)]
    for mc in range(2):
        st = xio[mc]
        S.dma("sp", st.t[:, :], mem_d[mc * 128:(mc + 1) * 128, :], W=[st])
        for half in range(2):
            ps = PS()
            for j in range(4):
                c = half * 4 + j
                tr(ps, ps.t[:, j * 128:(j + 1) * 128], st.t[:, c * 128:(c + 1) * 128], identF, R=[st, cstF],
                   sig=(j == 3))
            for j in range(4):
                c = half * 4 + j
                cp("dve", memT[c].t[:, mc * 128:(mc + 1) * 128], ps.t[:, j * 128:(j + 1) * 128], R=[ps], W=[memT[c]])
    for l in range(L if STAGE >= 3 else 0):
        mb = [allocB() for _ in range(8)]
        rms_to_bf(memT, l * V128_L + O_NMEM, mb, n=MEM)
        KSUB = int(os.environ.get("KSUB", "9"))
        for i in range(4 if KSUB >= 3 else (2 if KSUB == 2 else 0)):
            slot = load_slab("wmkv%d_%d" % (i, l))
            if i < 2:
                for j in range(4):
                    cidx = i * 4 + j
                    ps = PS()
                    proj_fm(slot, 384, j * 96, 96, ps, mb, n=MEM)
                    act(MKT[l].t[:96, cidx * MEM:(cidx + 1) * MEM], ps.t[:96, :MEM], AF.Identity, R=[ps], W=[MKT[l]],
                        scale=float(192 ** -0.5))
            else:
                half = i - 2
                for mc in range(2):
                    ps = PS()
                    for kc in range(8):
                        mm(ps, ps.t[:, :384], mb[kc].t[:, mc * 128:(mc + 1) * 128],
                           slot.t[:, kc * 384:(kc + 1) * 384], kc == 0, kc == 7, R=[slot, mb[kc]])
                    cp("act", MV[l].t[:, mc * MIX + half * 384:mc * MIX + half * 384 + 384], ps.t[:, :384],
                       R=[ps], W=[MV[l]])
        for t in mb:
            freeB(t)
        if KSUB >= 4:
            S.dma("sp", mkt_scr[l].t[:, :], MKT_one.t[:, :], R=[MKT_one], W=[mkt_scr[l]])
            S.dma("sp", mv_scr[l].t[:, :], MV_one.t[:, :], R=[MV_one], W=[mv_scr[l]])
    for t in memT:
        freeF(t)

    mix_iter = [0]
    if STAGE >= 4:
        S.dma("pool", wrgT[0].t[:96, :], wrg_d[0, :, :], W=[wrgT[0]])

    def mixer(l):
        vb = l * V128_L
        cb = l * V96_L
        rms_to_bf(X, vb + O_NMIX, Hb)
        S.dma("sp", MKT_one.t[:, :], mkt_scr[l].t[:, :], R=[mkt_scr[l]], W=[MKT_one])
        S.dma("sp", MV_one.t[:, :], mv_scr[l].t[:, :], R=[mv_scr[l]], W=[MV_one])
        cp("act", Sb_one.t[:96, :], Sst_all[:, l, :], R=Sst[l], W=[Sb_one])

        wrg = wrgT[mix_iter[0] % 2]
        outA = [None] * 8
        slotA = None
        xrs, xrbs = [], []
        for h in range(8):
            if h % 4 == 0:
                slotA = load_slab("winA%d_%d" % (h // 4, l))
            ps = PS()
            proj_fm(slotA, 384, (h % 4) * 96, 96, ps, Hb)
            zb = zbuf[h % 2]
            cp("pool", zb.t[:, 0:3], rhalo[l][h].t, R=[rhalo[l][h]], W=[zb])
            cp("act", zb.t[:, 3:3 + T], ps.t[:96, :T], R=[ps], W=[zb])
            cp("pool", rhalo[l][h].t, zb.t[:, T:T + 3], R=[zb], W=[rhalo[l][h]])
            xr = allocF()

            def cw(k):
                return V96.t[:, cb + O_CW0 + 8 * k + h:cb + O_CW0 + 8 * k + h + 1]
            ts("dve", xr.t[:96, :], zb.t[:, 3:3 + T], cw(3), V96.t[:, cb + O_CB + h:cb + O_CB + h + 1],
               ALU.mult, ALU.add, R=[zb, V96], W=[xr])
            for k in (2, 1, 0):
                stt("dve", xr.t[:96, :], zb.t[:, k:k + T], cw(k), xr.t[:96, :], ALU.mult, ALU.add,
                    R=[zb, V96, xr], W=[xr])
            xrb = allocB()
            cp("act", xrb.t[:96, :], xr.t[:96, :], R=[xr], W=[xrb])
            xrs.append(xr)
            xrbs.append(xrb)
        def rg2(h):
            xr, xrb = xrs[h], xrbs[h]
            psa = PS()
            mm(psa, psa.t[:96, :T], wrg.t[:96, h * 96:(h + 1) * 96], xrb.t[:96, :T], True, True, R=[wrg, xrb])
            psx = PS()
            mm(psx, psx.t[:96, :T], wrg.t[:96, (8 + h) * 96:(9 + h) * 96], xrb.t[:96, :T], True, True, R=[wrg, xrb])
            freeB(xrb)
            ta = allocF()
            ti = allocF()
            act(ta.t[:96, :], psa.t[:96, :T], AF.Tanh, R=[psa, H96], W=[ta],
                bias=H96.t[:, cb + O_BA + h:cb + O_BA + h + 1], scale=0.5)
            act(ti.t[:96, :], psx.t[:96, :T], AF.Tanh, R=[psx, H96], W=[ti],
                bias=H96.t[:, cb + O_BX + h:cb + O_BX + h + 1], scale=0.5)
            aa = allocF()
            e2 = allocF()
            hcl = HCL96.t[:, cb + O_LAM + h:cb + O_LAM + h + 1]
            cl = CL96.t[:, cb + O_LAM + h:cb + O_LAM + h + 1]
            act(aa.t[:96, :], ta.t[:96, :], AF.Exp, R=[ta, HCL96], W=[aa], bias=hcl, scale=hcl)
            act(e2.t[:96, :], ta.t[:96, :], AF.Exp, R=[ta, CL96], W=[e2], bias=cl, scale=cl)
            stt("dve", ti.t[:96, :], ti.t[:96, :], 1.0, xr.t[:96, :], ALU.add, ALU.mult, R=[ti, xr], W=[ti])
            ts("dve", e2.t[:96, :], e2.t[:96, :], -0.25, 0.25, ALU.mult, ALU.add, R=[e2], W=[e2])
            ts("dve", e2.t[:96, :], e2.t[:96, :], 1e-30, None, ALU.max, None, R=[e2], W=[e2])
            act(e2.t[:96, :], e2.t[:96, :], AF.Ln, R=[e2], W=[e2])
            act(e2.t[:96, :], e2.t[:96, :], AF.Exp, R=[e2], W=[e2], scale=0.5)
            tt("dve", e2.t[:96, :], e2.t[:96, :], ti.t[:96, :], ALU.mult, R=[e2, ti], W=[e2])
            S.op("dve", lambda e: e.tensor_tensor_scan(out=ta.t[:96, :], data0=aa.t[:96, :], data1=e2.t[:96, :],
                                                       initial=hst[l][h].t, op0=ALU.mult, op1=ALU.add),
                 R=[aa, e2, hst[l][h]], W=[ta])
            oa = allocB()
            cp("act", oa.t[:96, :], ta.t[:96, :], R=[ta], W=[oa])
            cp("pool", hst[l][h].t, ta.t[:96, T - 1:T], R=[ta], W=[hst[l][h]])
            outA[h] = oa
            for t in (xr, ta, ti, aa, e2):
                freeF(t)

        mix_iter[0] += 1
        if mix_iter[0] < NT * L:
            S.dma("pool", wrgT[mix_iter[0] % 2].t[:96, :], wrg_d[mix_iter[0] % L, :, :], W=[wrgT[mix_iter[0] % 2]])

        wdec = load_slab("wdec_%d" % l)
        ps = PS()
        proj_fm(wdec, 16, 0, 16, ps, Hb)
        cp("act", zd.t[:16, :], ps.t[:16, :T], R=[ps], W=[zd])
        Epos, Eneg = [], []
        for hh in range(4):
            ps = PS()
            mm(ps, ps.t[:96, :T], wdcyB.t[:16, l * 384 + hh * 96:l * 384 + (hh + 1) * 96], zd.t[:16, :T], True, True,
               R=[wdcyB, zd])
            e = allocF()
            act(e.t[:96, :], ps.t[:96, :T], AF.Exp, R=[ps, N96], W=[e],
                bias=N96.t[:, cb + O_BD + hh:cb + O_BD + hh + 1], scale=-1.0)
            act(e.t[:96, :], e.t[:96, :], AF.Ln, R=[e], W=[e], bias=1.0)
            cs = allocF()
            S.op("dve", lambda en, cs=cs, e=e: en.tensor_tensor_scan(out=cs.t[:96, :], data0=smask[:96, :],
                                                                     data1=e.t[:96, :], initial=0.0,
                                                                     op0=ALU.mult, op1=ALU.add),
                 R=[cstF, e], W=[cs], dur=0.8)
            ep = allocB()
            en_ = allocB()
            act(ep.t[:96, :], cs.t[:96, :], AF.Exp, R=[cs], W=[ep], scale=-1.0 / 16.0,
                bias=float(np.log(96.0 ** -0.5)))
            act(en_.t[:96, :], cs.t[:96, :], AF.Exp, R=[cs], W=[en_], scale=1.0 / 16.0)
            act(dcy[hh].t[:96, 0:4], cs.t[:96, C_GLA - 1::C_GLA], AF.Exp, R=[cs], W=[dcy[hh]], scale=-1.0 / 16.0)
            Epos.append(ep)
            Eneg.append(en_)
            freeF(e)
            freeF(cs)
        slot = load_slab("win2_%d" % l)
        qe = []
        for hh in range(4):
            ps = PS()
            proj_fm(slot, 384, hh * 96, 96, ps, Hb)
            q = allocB()
            tt("dve", q.t[:96, :], ps.t[:96, :T], Epos[hh].t[:96, :], ALU.mult, R=[ps, Epos[hh]], W=[q])
            qe.append(q)
            freeB(Epos[hh])
        slot = load_slab("win3_%d" % l)
        ke = []
        for hh in range(4):
            ps = PS()
            proj_fm(slot, 384, hh * 96, 96, ps, Hb)
            k = allocB()
            tt("dve", k.t[:96, :], ps.t[:96, :T], Eneg[hh].t[:96, :], ALU.mult, R=[ps, Eneg[hh]], W=[k])
            ke.append(k)
            freeB(Eneg[hh])
        for half in range(2):
            slot = load_slab("win%d_%d" % (4 + half, l))
            for s in range(4):
                ps = PS()
                for kc in range(8):
                    mm(ps, ps.t[:, :384], Hb[kc].t[:, s * 128:(s + 1) * 128], slot.t[:, kc * 384:(kc + 1) * 384],
                       kc == 0, kc == 7, R=[slot, Hb[kc]])
                cp("act", vt[s].t[:, half * 384:(half + 1) * 384], ps.t[:, :384], R=[ps], W=[vt[s]])
        for half in range(2):
            slot = load_slab("win%d_%d" % (6 + half, l))
            for j in range(3):
                c = half * 3 + j
                ps = PS()
                proj_fm(slot, 384, j * 128, 128, ps, Hb)
                tg = allocF()
                act(tg.t[:, :], ps.t[:, :T], AF.Tanh, R=[ps], W=[tg], scale=0.5)
                stt("dve", tg.t[:, :], tg.t[:, :], 1.0, ps.t[:, :T], ALU.add, ALU.mult, R=[tg, ps], W=[tg])
                ts("pool", sgg3a[:, c, :], tg.t[:, :], H128.t[:, vb + O_GLAN + c:vb + O_GLAN + c + 1], None,
                   ALU.mult, None, R=[tg, H128], W=[sgg3])
                freeF(tg)

        kts = [None] * 4
        scs = [None] * 4
        pskvs = [None] * 4

        def gla_A(s):
            tsl = slice(s * 128, (s + 1) * 128)
            pst = PS()
            pstb = pst.t[:, :].bitcast(BF16)
            for hh in range(4):
                tr(pst, pstb[:, hh * 96:(hh + 1) * 96], ke[hh].t[:96, tsl], identB[:96, :96], R=[ke[hh], cstB],
                   sig=(hh == 3))
            kt = ktt[s % 2]
            cp("act", kt.t[:, :], pstb[:, 0:384], R=[pst], W=[kt])
            pss = PS()
            for hh in range(4):
                mm(pss, pss.t[:, hh * 128:(hh + 1) * 128], ke[hh].t[:96, tsl], qe[hh].t[:96, tsl], True, True,
                   R=[ke[hh], qe[hh]])
            sc = scT[s % 2]
            tt("dve", sc.t[:, :], pss.t[:, :], tri4B, ALU.mult, R=[pss, cstB], W=[sc])
            kts[s] = kt
            scs[s] = sc

        def gla_KV(s):
            kt = kts[s]
            pk = []
            for hp in range(2):
                pskv = PS()
                for hh in (2 * hp, 2 * hp + 1):
                    oc = (hh % 2) * 192
                    mm(pskv, pskv.t[:96, oc:oc + 192], kt.t[:, hh * 96:(hh + 1) * 96],
                       vt[s].t[:, hh * 192:(hh + 1) * 192], True, True, R=[kt, vt[s]])
                pk.append(pskv)
            pskvs[s] = pk

        ons = [None] * 4

        def gla_B(s):
            tsl = slice(s * 128, (s + 1) * 128)
            sc = scs[s]
            on = onb[s % 2]
            sq = ssq[s % 2]
            S.op("pool", lambda e: e.memset(sq.t[:, 0:4], 0.0), W=[sq])
            psos = []
            for hp in range(2):
                pso = PS()
                psos.append(pso)
                for hh in (2 * hp, 2 * hp + 1):
                    oc = (hh % 2) * 192
                    mm(pso, pso.t[:, oc:oc + 192], sc.t[:, hh * 128:(hh + 1) * 128], vt[s].t[:, hh * 192:(hh + 1) * 192],
                       True, False, R=[sc, vt[s]])
                    mm(pso, pso.t[:, oc:oc + 192], qe[hh].t[:96, tsl], Sb_t[l].t[:96, hh * 192:(hh + 1) * 192],
                       False, True, R=[qe[hh], Sb_t[l]])
                for hh in (2 * hp, 2 * hp + 1):
                    oc = (hh % 2) * 192
                    act(junk.t[:, :], pso.t[:, oc:oc + 192], AF.Square, R=[pso], W=[junk, sq],
                        accum_out=sq.t[:, hh:hh + 1])
            act(sq.t[:, 4:8], sq.t[:, 0:4], AF.Ln, R=[sq, cstF], W=[sq], scale=1.0 / 192.0, bias=epsc)
            act(sq.t[:, 4:8], sq.t[:, 4:8], AF.Exp, R=[sq], W=[sq], scale=-0.5)
            for hh in range(4):
                pso = psos[hh // 2]
                oc = (hh % 2) * 192
                ts("dve", on.t[:, hh * 192:(hh + 1) * 192], pso.t[:, oc:oc + 192], sq.t[:, 4 + hh:5 + hh], None,
                   ALU.mult, None, R=[pso, sq], W=[on])
            ons[s] = on

        def gla_U(s):
            for hp in range(2):
                pskv = pskvs[s][hp]
                for hh in (2 * hp, 2 * hp + 1):
                    oc = (hh % 2) * 192
                    st_ = Sst[l][hh]
                    ts("pool", st_.t, st_.t, dcy[hh].t[:96, s:s + 1], None, ALU.mult, None, R=[st_, dcy[hh]], W=[st_])
                    stt("dve", st_.t, pskv.t[:96, oc:oc + 192], dcy[hh].t[:96, s:s + 1], st_.t, ALU.mult, ALU.add,
                        R=[pskv, dcy[hh], st_], W=[st_])
            cp("act", Sb_t[l].t[:96, :], Sst_all[:, l, :], R=Sst[l], W=[Sb_t[l]])

        def gla_C(s):
            tsl = slice(s * 128, (s + 1) * 128)
            on = ons[s]
            pst2 = PS()
            pst2b = pst2.t[:, :].bitcast(BF16)
            for c in range(6):
                tr(pst2, pst2b[:, c * 128:(c + 1) * 128], on.t[:, c * 128:(c + 1) * 128], identB, R=[on, cstB],
                   sig=(c == 5))
            tt("dve", outB3a[:, :, tsl], pst2b[:, 0:768].rearrange("p (c i) -> p c i", c=6), sgg3a[:, :, tsl],
               ALU.mult, R=[pst2, sgg3], W=[outB3])

        mqT = [None] * 8
        mq_slot = [None]

        def mem_Q(c):
            if c % 4 == 0:
                mq_slot[0] = load_slab("win%d_%d" % (8 + c // 4, l))
            ps = PS()
            proj_fm(mq_slot[0], 384, (c % 4) * 96, 96, ps, Hb)
            m = allocB()
            cp("act", m.t[:96, :], ps.t[:96, :T], R=[ps], W=[m])
            mqT[c] = m

        pTs = [None] * 4
        outC = [None] * 8

        def mem_1(h):
            pT = [allocB(), allocB()]
            for mc in range(2):
                ps = PS()
                for j in range(2):
                    cidx = 2 * h + j
                    mm(ps, ps.t[:, :T], MKT[l].t[:96, cidx * MEM + mc * 128:cidx * MEM + (mc + 1) * 128],
                       mqT[cidx].t[:96, :T], j == 0, j == 1, R=[MKT[l], mqT[cidx]])
                act(pT[mc].t[:, :], ps.t[:, :T], AF.Exp, R=[ps], W=[pT[mc]])
            pTs[h] = pT

        def mem_2(h):
            pT = pTs[h]
            psr = PS()
            for mc in range(2):
                mm(psr, psr.t[:96, :T], onesB[:, :96], pT[mc].t[:, :T], mc == 0, mc == 1, R=[cstB, pT[mc]])
            rinv = allocF()
            S.op("dve", lambda e: e.reciprocal(out=rinv.t[:96, :], in_=psr.t[:96, :T]), R=[psr], W=[rinv])
            for j in range(2):
                pso = PS()
                for mc in range(2):
                    c0 = mc * MIX + h * 192 + j * 96
                    mm(pso, pso.t[:96, :T], MV[l].t[:, c0:c0 + 96], pT[mc].t[:, :T], mc == 0, mc == 1,
                       R=[MV[l], pT[mc]])
                oc_ = allocB()
                tt("dve", oc_.t[:96, :], pso.t[:96, :T], rinv.t[:96, :], ALU.mult, R=[pso, rinv], W=[oc_])
                outC[2 * h + j] = oc_
            freeF(rinv)
            for t in pT:
                freeB(t)
            freeB(mqT[2 * h])
            freeB(mqT[2 * h + 1])

        gla_A(0)
        gla_A(1)
        for c in range(0, 4):
            mem_Q(c)
        rg2(0)
        gla_B(0)
        gla_KV(0)
        gla_U(0)
        for c in range(4, 8):
            mem_Q(c)
        rg2(1)
        gla_A(2)
        gla_B(1)
        gla_KV(1)
        gla_U(1)
        mem_1(0)
        rg2(2)
        gla_C(0)
        gla_A(3)
        mem_1(1)
        rg2(3)
        gla_B(2)
        gla_KV(2)
        gla_U(2)
        mem_2(0)
        rg2(4)
        gla_C(1)
        mem_1(2)
        gla_B(3)
        gla_KV(3)
        gla_U(3)
        rg2(5)
        mem_2(1)
        gla_C(2)
        mem_1(3)
        rg2(6)
        mem_2(2)
        gla_C(3)
        rg2(7)
        mem_2(3)
        for t in qe + ke:
            freeB(t)

        merged = []
        for dc in range(8):
            wg = load_slab("wg%d_%d" % (dc, l))
            wb = load_slab("wb%d_%d" % (dc, l))
            acc = None
            for b in range(3):
                psg = PS()
                proj_fm(wg, 384, b * 128, 128, psg, Hb)
                tg = allocF()
                act(tg.t[:, :], psg.t[:, :T], AF.Tanh, R=[psg], W=[tg], scale=0.5)
                psp = PS()
                if b == 0:
                    ch = [(96, wb.t[:96, i * 128:(i + 1) * 128], outA[i].t[:96, :T], outA[i]) for i in range(8)]
                elif b == 1:
                    ch = [(128, wb.t[:, (8 + i) * 128:(9 + i) * 128], outB3a[:, i, :], outB3) for i in range(6)]
                else:
                    ch = [(96, wb.t[:96, (14 + i) * 128:(15 + i) * 128], outC[i].t[:96, :T], outC[i]) for i in range(8)]
                for i, (K, lh, rh, rt) in enumerate(ch):
                    mm(psp, psp.t[:, :T], lh, rh, i == 0, i == len(ch) - 1, R=[wb, rt])
                stt("dve", tg.t[:, :], tg.t[:, :], 1.0, psp.t[:, :T], ALU.add, ALU.mult, R=[tg, psp], W=[tg])
                if acc is None:
                    acc = tg
                else:
                    tt("pool", acc.t[:, :], acc.t[:, :], tg.t[:, :], ALU.add, R=[acc, tg], W=[acc])
                    freeF(tg)
            mg = allocB()
            cp("act", mg.t[:, :], acc.t[:, :], R=[acc], W=[mg])
            freeF(acc)
            merged.append(mg)
        for t in outA + outC:
            freeB(t)

        slot = None
        for dc in range(8):
            if dc % 2 == 0:
                slot = load_slab("wout%d_%d" % (dc // 2, l))
            ps = PS()
            proj_fm(slot, 256, (dc % 2) * 128, 128, ps, merged)
            stt("dve", X[dc].t, ps.t[:, :T], 0.5, X[dc].t, ALU.mult, ALU.add, R=[ps, X[dc]], W=[X[dc]])
        for t in merged:
            freeB(t)

    def ffn(l):
        vb = l * V128_L
        rms_to_bf(X, vb + O_NFFN, Hb)
        Pm = []
        for j in range(12):
            slot = load_slab("wup%d_%d" % (j, l))
            for pi in range(2):
                g = 2 * j + pi
                accs = []
                for vi in range(2):
                    chn = g + 24 * vi
                    ps = PS()
                    proj_fm(slot, 512, pi * 256 + vi * 128, 128, ps, Hb)
                    acc = allocF()
                    w0 = V128.t[:, vb + O_FW0 + chn:vb + O_FW0 + chn + 1]
                    w1 = V128.t[:, vb + O_FW1 + chn:vb + O_FW1 + chn + 1]
                    w2 = V128.t[:, vb + O_FW2 + chn:vb + O_FW2 + chn + 1]
                    bb = V128.t[:, vb + O_FB + chn:vb + O_FB + chn + 1]
                    act(acc.t[:, :], ps.t[:, :T], AF.Identity, R=[ps, V128], W=[acc], bias=bb, scale=w2)
                    stt("dve", acc.t[:, 1:T], ps.t[:, 0:T - 1], w1, acc.t[:, 1:T], ALU.mult, ALU.add,
                        R=[ps, V128, acc], W=[acc])
                    stt("dve", acc.t[:, 2:T], ps.t[:, 0:T - 2], w0, acc.t[:, 2:T], ALU.mult, ALU.add,
                        R=[ps, V128, acc], W=[acc])
                    fht = fh[l][chn]
                    stt("dve", acc.t[:, 0:2], fht.t[:, 0:2], w0, acc.t[:, 0:2], ALU.mult, ALU.add,
                        R=[fht, V128, acc], W=[acc])
                    stt("dve", acc.t[:, 0:1], fht.t[:, 1:2], w1, acc.t[:, 0:1], ALU.mult, ALU.add,
                        R=[fht, V128, acc], W=[acc])
                    cp("act", fht.t[:, 0:2], ps.t[:, T - 2:T], R=[ps], W=[fht])
                    accs.append(acc)
                act(accs[0].t[:, :], accs[0].t[:, :], AF.Gelu_apprx_tanh, R=[accs[0]], W=[accs[0]])
                pm = allocB()
                tt("pool", pm.t[:, :], accs[0].t[:, :], accs[1].t[:, :], ALU.mult, R=accs, W=[pm])
                Pm.append(pm)
                freeF(accs[0])
                freeF(accs[1])
        for dc in range(8):
            slot = load_slab("wdn%d_%d" % (dc, l))
            ps = PS()
            for kc in range(24):
                mm(ps, ps.t[:, :T], slot.t[:, kc * 128:(kc + 1) * 128], Pm[kc].t[:, :T], kc == 0, kc == 23,
                   R=[slot, Pm[kc]])
            tt("dve", X[dc].t, ps.t[:, :T], X[dc].t, ALU.add, R=[ps, X[dc]], W=[X[dc]])
        for t in Pm:
            freeB(t)

    out_tiles = []
    for it in range(NT):
        t0 = it * T
        for s in range(4):
            st = xio[s % 2]
            S.dma("sp", st.t[:, :], x_d[t0 + s * 128:t0 + (s + 1) * 128, :], W=[st])
            for half in range(2):
                ps = PS()
                for j in range(4):
                    c = half * 4 + j
                    tr(ps, ps.t[:, j * 128:(j + 1) * 128], st.t[:, c * 128:(c + 1) * 128], identF, R=[st, cstF],
                       sig=(j == 3))
                cp("act", Xall[:, half * 4:half * 4 + 4, s * 128:(s + 1) * 128],
                   ps.t[:, :].rearrange("p (c i) -> p c i", c=4), R=[ps], W=X[half * 4:half * 4 + 4])
        for l in range(L if STAGE >= 4 else 0):
            if STAGE != 6:
                mixer(l)
            if STAGE != 5:
                ffn(l)
        psn = PS()
        for c in range(8):
            sq = allocF()
            act(sq.t[:, :], X[c].t, AF.Square, R=[X[c]], W=[sq])
            mm(psn, psn.t[:, :T], onesF, sq.t[:, :], c == 0, c == 7, R=[cstF, sq], sig=True)
            freeF(sq)
        rstd = allocF()
        act(rstd.t[:, :], psn.t[:, :T], AF.Ln, R=[psn, cstF], W=[rstd], scale=1.0 / D, bias=epsc)
        act(rstd.t[:, :], rstd.t[:, :], AF.Exp, R=[rstd], W=[rstd], scale=-0.5)
        for c in range(8):
            stt("dve", X[c].t, X[c].t, V128.t[:, O_FINAL + c:O_FINAL + c + 1], rstd.t[:, :], ALU.mult, ALU.mult,
                R=[X[c], V128, rstd], W=[X[c]])
        freeF(rstd)
        for s in range(4):
            st = xio[s % 2]
            for half in range(2):
                ps = PS()
                for j in range(4):
                    c = half * 4 + j
                    tr(ps, ps.t[:, j * 128:(j + 1) * 128], Xall[:, c, s * 128:(s + 1) * 128], identF, R=[X[c], cstF],
                       sig=(j == 3))
                cp("act", st.t[:, half * 512:(half + 1) * 512], ps.t[:, :], R=[ps], W=[st])
            S.dma("sp", y_d[t0 + s * 128:t0 + (s + 1) * 128, :], st.t[:, :], R=[st])
            out_tiles.append(st)

    S.schedule_and_emit(reorder=(os.environ.get("KNOREORDER") is None))
    for st in xio:
        if st.dsem is not None and st.dcnt:
            nc.sync.wait_ge(st.dsem, st.dcnt)
    build.stats = (S.ninst, S.nwait, nc.sbuf_bytes_remaining, S.sim_time)
    return nc


_CACHE = {}


def prep_inputs(inp, L):
    wts = pack_weights(inp, L)
    p128, p96 = pack_small(inp, L)
    cst = make_consts()
    return wts, p128, p96, cst


def kernel(**inputs):
    inp = {k: np.asarray(v) for k, v in inputs.items()}
    x = inp["x"]
    B, S_TOK, _ = x.shape
    L = inp["w_in"].shape[0]
    wts, p128, p96, cst = prep_inputs(inp, L)
    nc = build(L, S_TOK)
    in_maps = []
    for b in range(B):
        m = {"x": np.ascontiguousarray(x[b]), "mem": np.ascontiguousarray(inp["mem"][b]),
             "p128": p128, "p96": p96, "cst": cst}
        m.update(wts)
        in_maps.append(m)
    res = run_bass_kernel_spmd(nc, in_maps, core_ids=list(range(B)))
    out = np.stack([np.asarray(r["y"]) for r in res.results], axis=0)
    return out.astype(np.float32)
```

```python
import numpy as np
import concourse.bass as bass
import concourse.mybir as mybir
from concourse.bass_utils import run_bass_kernel_spmd

F32 = mybir.dt.float32
BF16 = mybir.dt.bfloat16
AF = mybir.ActivationFunctionType
ALU = mybir.AluOpType

D = 1024
NL = 4
SEQ = 8192
NB = 8
MEM = 256
MIX = 768
T = 512
EPS = 1e-6
NIN = 6928
SLOTW = 4096
C_GLA = 128

V128_L = 222
O_NMIX, O_NFFN, O_NMEM, O_GLAN, O_FW0, O_FW1, O_FW2, O_FB = 0, 8, 16, 24, 30, 78, 126, 174
O_FINAL = 4 * V128_L
R128 = O_FINAL + 8
V96_L = 68
O_CW0, O_CB, O_BA, O_BX, O_LAM, O_BD = 0, 32, 40, 48, 56, 64
R96 = 4 * V96_L


class Tile:
    __slots__ = ("t", "w", "r", "dsem", "dcnt", "name")

    def __init__(self, t, name=""):
        self.t = t
        self.w = None
        self.r = {}
        self.dsem = None
        self.dcnt = 0
        self.name = name


class Sched:
    def __init__(self, nc):
        self.nc = nc
        self.engs = {"pe": nc.tensor, "dve": nc.vector, "act": nc.scalar,
                     "pool": nc.gpsimd, "sp": nc.sync}
        self.sem = {k: nc.alloc_semaphore("sem_" + k) for k in self.engs}
        self.cnt = {k: 0 for k in self.engs}
        self.known = {k: {} for k in self.engs}
        self.semobj = {}
        self.nwait = 0
        self.ninst = 0
        self.out_waits = []
        self._uid = 0
        self.recording = True
        self.rec = []
        self._grp = []
        self._grp_dur = 0.0
        self.sim_time = 0.0
        import os as _os
        self.crit = _os.environ.get("KCRIT", "1") == "1"
        for k, s in self.sem.items():
            self.semobj[id(s)] = s

    def uid(self):
        self._uid += 1
        return self._uid

    def _wait(self, E, deps):
        eng = self.engs[E]
        kn = self.known[E]
        best = {}
        for (s, v) in deps:
            k = id(s)
            if kn.get(k, 0) >= v:
                continue
            if best.get(k, (None, 0))[1] < v:
                best[k] = (s, v)
        for k, (s, v) in best.items():
            if s is self.sem["pe"]:
                assert v <= self.cnt["pe"], "wait on an un-signalled PE group (deadlock)"
            eng.wait_ge(s, v)
            kn[k] = v
            self.nwait += 1

    def _deps(self, E, R, W):
        own = self.sem[E]
        deps = []
        for t in R:
            if t.w is not None:
                if not (E == "pe" and t.w[0] is own):
                    deps.append(t.w)
        for t in W:
            if t.w is not None:
                if not (E == "pe" and t.w[0] is own):
                    deps.append(t.w)
            for k, (s, v) in t.r.items():
                if s is own:
                    continue
                deps.append((s, v))
        return deps

    def op(self, E, fn, R=(), W=(), sig=True, dur=0.5):
        if not self.recording:
            return self._op_now(E, fn, R, W, sig)
        if E == "pe":
            self._grp.append((fn, list(R), list(W)))
            self._grp_dur += dur
            if sig:
                Rs, Ws = [], []
                for (_, r, w) in self._grp:
                    Rs += r
                    Ws += w
                self.rec.append(("pe", self._grp, Rs, Ws, self._grp_dur))
                self._grp = []
                self._grp_dur = 0.0
        else:
            self.rec.append((E, [(fn, list(R), list(W))], list(R), list(W), dur))

    def dma(self, E, out_ap, in_ap, R=(), W=(), dur=6.0):
        if not self.recording:
            return self._dma_now(E, out_ap, in_ap, R, W)
        self.rec.append(("dma:" + E, (out_ap, in_ap), list(R), list(W), dur))

    def schedule_and_emit(self, reorder=True):
        import heapq
        rec = self.rec
        n = len(rec)
        assert not self._grp
        succ = [[] for _ in range(n)]
        indeg = [0] * n
        lastw = {}
        readers = {}
        for i, (E, body, R, W, dur) in enumerate(rec):
            preds = set()
            for t in R:
                k = id(t)
                if k in lastw:
                    preds.add(lastw[k])
            for t in W:
                k = id(t)
                if k in lastw:
                    preds.add(lastw[k])
                for r in readers.get(k, ()):
                    preds.add(r)
            preds.discard(i)
            for p in preds:
                succ[p].append(i)
            indeg[i] = len(preds)
            for t in W:
                k = id(t)
                lastw[k] = i
                readers[k] = []
            for t in R:
                k = id(t)
                if lastw.get(k) != i:
                    readers.setdefault(k, []).append(i)
        order = []
        if not reorder:
            order = list(range(n))
        else:
            engs = ["pe", "dve", "act", "pool", "sp"]
            efree = {e: 0.0 for e in engs}
            fut = {e: [] for e in engs}
            now = {e: [] for e in engs}
            rtime = [0.0] * n

            def eng_of(i):
                E = rec[i][0]
                return E[4:] if E.startswith("dma:") else E
            prio = list(range(n))
            if self.crit:
                bl = [0.0] * n
                for i in range(n - 1, -1, -1):
                    m = 0.0
                    for j in succ[i]:
                        if bl[j] > m:
                            m = bl[j]
                    bl[i] = rec[i][4] + m
                prio = [-b for b in bl]
            for i in range(n):
                if indeg[i] == 0:
                    heapq.heappush(fut[eng_of(i)], (0.0, i))
            done = 0
            while done < n:
                best = None
                for e in engs:
                    f, nw = fut[e], now[e]
                    while f and f[0][0] <= efree[e]:
                        k_ = heapq.heappop(f)[1]
                        heapq.heappush(nw, (prio[k_], k_))
                    if nw:
                        cand = (efree[e], nw[0][1], e, True)
                    elif f:
                        cand = (f[0][0], f[0][1], e, False)
                    else:
                        continue
                    if best is None or cand[:2] < best[:2]:
                        best = cand
                st, i, e, from_now = best
                if from_now:
                    heapq.heappop(now[e])
                else:
                    heapq.heappop(fut[e])
                E, body, R, W, dur = rec[i]
                if E.startswith("dma:"):
                    efree[e] = st + 0.15
                    fin = st + dur
                else:
                    efree[e] = st + dur
                    fin = st + dur
                order.append(i)
                done += 1
                for j in succ[i]:
                    indeg[j] -= 1
                    if rtime[j] < fin + 0.1:
                        rtime[j] = fin + 0.1
                    if indeg[j] == 0:
                        heapq.heappush(fut[eng_of(j)], (rtime[j], j))
            self.sim_time = max(efree.values())
        self.recording = False
        for i in order:
            E, body, R, W, dur = rec[i]
            if E.startswith("dma:"):
                self._dma_now(E[4:], body[0], body[1], R, W)
            else:
                m = len(body)
                for k, (fn, r, w) in enumerate(body):
                    self._op_now(E, fn, r, w, (k == m - 1) if E == "pe" else True)

    def _op_now(self, E, fn, R=(), W=(), sig=True):
        self._wait(E, self._deps(E, R, W))
        ins = fn(self.engs[E])
        self.ninst += 1
        s = self.sem[E]
        if sig:
            self.cnt[E] += 1
            ins.then_inc(s, 1)
            v = self.cnt[E]
        else:
            v = self.cnt[E] + 1
        for t in R:
            if t.r.get(id(s), (s, 0))[1] < v:
                t.r[id(s)] = (s, v)
        for t in W:
            t.w = (s, v)
            t.r = {}
        return ins

    def _dma_now(self, E, out_ap, in_ap, R=(), W=()):
        self._wait(E, self._deps(E, R, W))
        ins = self.engs[E].dma_start(out=out_ap, in_=in_ap)
        self.ninst += 1
        for t in list(W) + list(R):
            if t.dsem is None:
                t.dsem = self.nc.alloc_semaphore("dsem%d" % self.uid())
        tt = W[0] if len(W) else R[0]
        tt.dcnt += 16
        ins.then_inc(tt.dsem, 16)
        for t in W:
            t.w = (tt.dsem, tt.dcnt)
            t.r = {}
        for t in R:
            t.r[id(tt.dsem)] = (tt.dsem, tt.dcnt)
        return tt


def _colslab(W, cols):
    n = len(cols)
    sub = W[:, cols]
    return np.ascontiguousarray(sub.reshape(8, 128, n).transpose(1, 0, 2).reshape(128, 8 * n))


def pack_weights(inp, L):
    w_in = inp["w_in"]
    win_s = np.zeros((L, 18, 128, 3072), np.float32)
    wdec_s = np.zeros((L, 128, 128), np.float32)
    wmkv_s = np.zeros((L, 4, 128, 3072), np.float32)
    wout_s = np.zeros((L, 4, 128, 2048), np.float32)
    wup_s = np.zeros((L, 12, 128, 4096), np.float32)
    wdn_s = np.zeros((L, 8, 128, 3072), np.float32)
    wbr_s = np.zeros((L, 8, 128, 2816), np.float32)
    wrg_s = np.zeros((L, 96, 1536), np.float32)
    wdcy_s = np.zeros((L, 16, 384), np.float32)
    ar = np.arange
    for l in range(L):
        W = w_in[l]
        groups = []
        for s in range(2):
            groups.append(ar(0, 384) + s * 384)
        groups.append(ar(768, 1152))
        groups.append(ar(1152, 1536))
        for s in range(2):
            groups.append(ar(1536, 1920) + s * 384)
        for s in range(2):
            groups.append(ar(2304, 2688) + s * 384)
        for s in range(2):
            groups.append(ar(3088, 3472) + s * 384)
        for dc in range(8):
            groups.append(np.concatenate([3856 + b * 1024 + dc * 128 + ar(128) for b in range(3)]))
        assert len(groups) == 18
        for i, g in enumerate(groups):
            win_s[l, i] = _colslab(W, g)
        wdec_s[l] = _colslab(W, ar(3072, 3088))
        for i in range(4):
            wmkv_s[l, i] = _colslab(inp["w_mem_kv"][l], ar(384) + i * 384)
        for i in range(4):
            wout_s[l, i] = _colslab(inp["w_out"][l], ar(256) + i * 256)
        for j in range(12):
            cols = np.concatenate([np.concatenate([(2 * j + pi) * 128 + ar(128),
                                                   3072 + (2 * j + pi) * 128 + ar(128)]) for pi in range(2)])
            wup_s[l, j] = _colslab(inp["w_up"][l], cols)
        wd = inp["w_down"][l]
        for dc in range(8):
            sub = wd[:, dc * 128:(dc + 1) * 128]
            wdn_s[l, dc] = sub.reshape(24, 128, 128).transpose(1, 0, 2).reshape(128, 3072)
        wb = inp["w_branch"][l]
        for dc in range(8):
            blk = np.zeros((128, 22, 128), np.float32)
            a = wb[0][:, dc * 128:(dc + 1) * 128].reshape(8, 96, 128)
            blk[:96, 0:8] = a.transpose(1, 0, 2)
            b = wb[1][:, dc * 128:(dc + 1) * 128].reshape(6, 128, 128)
            blk[:, 8:14] = b.transpose(1, 0, 2)
            c = wb[2][:, dc * 128:(dc + 1) * 128].reshape(8, 96, 128)
            blk[:96, 14:22] = c.transpose(1, 0, 2)
            wbr_s[l, dc] = blk.reshape(128, 2816)
        rg = np.stack([inp["rg_w_a"][l], inp["rg_w_x"][l]], 0)
        wrg_s[l] = rg.transpose(2, 0, 1, 3).reshape(96, 1536)
        wdcy_s[l] = inp["gla_w_decay"][l]
    return dict(win_s=win_s, wdec_s=wdec_s, wmkv_s=wmkv_s, wout_s=wout_s, wup_s=wup_s,
                wdn_s=wdn_s, wbr_s=wbr_s, wrg_s=wrg_s, wdcy_s=wdcy_s)


def pack_small(inp, L):
    p128 = np.zeros((R128, 128), np.float32)
    p96 = np.zeros((R96, 96), np.float32)
    for l in range(L):
        b = l * V128_L
        p128[b + O_NMIX:b + O_NMIX + 8] = inp["norm_mix"][l].reshape(8, 128)
        p128[b + O_NFFN:b + O_NFFN + 8] = inp["norm_ffn"][l].reshape(8, 128)
        p128[b + O_NMEM:b + O_NMEM + 8] = inp["mem_norm"][l].reshape(8, 128)
        p128[b + O_GLAN:b + O_GLAN + 6] = inp["gla_norm"][l].reshape(6, 128)
        for k in range(3):
            o = (O_FW0, O_FW1, O_FW2)[k]
            p128[b + o:b + o + 48] = inp["ffn_conv_w"][l, k].reshape(48, 128)
        p128[b + O_FB:b + O_FB + 48] = inp["ffn_conv_b"][l].reshape(48, 128)
        c = l * V96_L
        for k in range(4):
            p96[c + O_CW0 + 8 * k:c + O_CW0 + 8 * k + 8] = inp["rnn_conv_w"][l, k].reshape(8, 96)
        p96[c + O_CB:c + O_CB + 8] = inp["rnn_conv_b"][l].reshape(8, 96)
        p96[c + O_BA:c + O_BA + 8] = inp["rg_b_a"][l].reshape(8, 96)
        p96[c + O_BX:c + O_BX + 8] = inp["rg_b_x"][l].reshape(8, 96)
        p96[c + O_LAM:c + O_LAM + 8] = inp["rg_lambda"][l].reshape(8, 96)
        p96[c + O_BD:c + O_BD + 4] = inp["gla_b_decay"][l].reshape(4, 96)
    p128[O_FINAL:O_FINAL + 8] = inp["final_norm"].reshape(8, 128)
    return p128, p96


def make_consts():
    ident = np.eye(128, dtype=np.float32)
    tri = np.triu(np.ones((128, 128), np.float32))
    tri4 = np.tile(tri, (1, 4))
    smask = np.ones((128, T), np.float32)
    smask[:, ::C_GLA] = 0.0
    ones = np.ones((128, 128), np.float32)
    epsc = np.full((128, 8), EPS, np.float32)
    return np.ascontiguousarray(np.concatenate([ident, ones, tri4, smask, epsc], axis=1))


def build(L=NL, S_TOK=SEQ, dbg=False):
    import os
    STAGE = int(os.environ.get("KSTAGE", "9"))
    NT = S_TOK // T
    nc = bass.Bass("TRN2", target_bir_lowering=False)
    S = Sched(nc)

    def dram(name, shape, kind="ExternalInput", dt=F32):
        return nc.dram_tensor(name, list(shape), dt, kind=kind).ap()

    x_d = dram("x", [S_TOK, D])
    mem_d = dram("mem", [MEM, D])
    p128_d = dram("p128", [R128, 128])
    p96_d = dram("p96", [R96, 96])
    cst_d = dram("cst", [128, 1288])
    win_d = dram("win_s", [L, 18, 128, 3072])
    wdec_d = dram("wdec_s", [L, 128, 128])
    wmkv_d = dram("wmkv_s", [L, 4, 128, 3072])
    wout_d = dram("wout_s", [L, 4, 128, 2048])
    wup_d = dram("wup_s", [L, 12, 128, 4096])
    wdn_d = dram("wdn_s", [L, 8, 128, 3072])
    wbr_d = dram("wbr_s", [L, 8, 128, 2816])
    wrg_d = dram("wrg_s", [L, 96, 1536])
    wdcy_d = dram("wdcy_s", [L, 16, 384])
    y_d = dram("y", [S_TOK, D], kind="ExternalOutput")

    def sb(name, shape, dt=F32):
        return nc.alloc_sbuf_tensor(name, list(shape), dt)

    cstF = Tile(sb("cstF", [128, 1288]), "cstF")
    identF = cstF.t[:, 0:128]
    onesF = cstF.t[:, 128:256]
    smask = cstF.t[:, 768:1280]
    epsc = cstF.t[:, 1280:1281]
    cstB = Tile(sb("cstB", [128, 768], BF16), "cstB")
    identB = cstB.t[:, 0:128]
    onesB = cstB.t[:, 128:256]
    tri4B = cstB.t[:, 256:768]

    V128 = Tile(sb("V128", [128, R128]), "V128")
    H128 = Tile(sb("H128", [128, R128]), "H128")
    V96 = Tile(sb("V96", [96, R96]), "V96")
    H96 = Tile(sb("H96", [96, R96]), "H96")
    N96 = Tile(sb("N96", [96, R96]), "N96")
    CL96 = Tile(sb("CL96", [96, R96]), "CL96")
    HCL96 = Tile(sb("HCL96", [96, R96]), "HCL96")
    CLM96 = Tile(sb("CLM96", [96, R96]), "CLM96")
    wdcyB = Tile(sb("wdcyB", [16, L * 384], BF16), "wdcyB")

    Xall = sb("Xall", [128, 8, T])
    X = [Tile(Xall[:, c, :], "X%d" % c) for c in range(8)]
    Hball = sb("Hball", [128, 8, T], BF16)
    Hb = [Tile(Hball[:, c, :], "Hb%d" % c) for c in range(8)]
    MKT_one = Tile(sb("MKT", [96, 8 * MEM], BF16), "MKT")
    MV_one = Tile(sb("MV", [128, 2 * MIX], BF16), "MV")
    MKT = [MKT_one for l in range(L)]
    MV = [MV_one for l in range(L)]
    mkt_scr = [Tile(nc.dram_tensor("mkt_scr%d" % l, [96, 8 * MEM], BF16, kind="Internal").ap(), "mkts") for l in range(L)]
    mv_scr = [Tile(nc.dram_tensor("mv_scr%d" % l, [128, 2 * MIX], BF16, kind="Internal").ap(), "mvs") for l in range(L)]
    Sst_all = sb("Sst", [96, L, MIX])
    Sst = [[Tile(Sst_all[:, l, hh * 192:(hh + 1) * 192], "S") for hh in range(4)] for l in range(L)]
    Sb_one = Tile(sb("Sb", [96, MIX], BF16), "Sb")
    Sb_t = [Sb_one for l in range(L)]
    rhalo_all = sb("rhalo", [96, L, 8, 3])
    rhalo = [[Tile(rhalo_all[:, l, h, :], "rh") for h in range(8)] for l in range(L)]
    hst_all = sb("hst", [96, L, 8])
    hst = [[Tile(hst_all[:, l, h:h + 1], "hst") for h in range(8)] for l in range(L)]
    fh_all = sb("fh", [128, L, 48, 2])
    fh = [[Tile(fh_all[:, l, c, :], "fh") for c in range(48)] for l in range(L)]
    outB3a = sb("outB3", [128, 6, T], BF16)
    outB3 = Tile(outB3a, "outB3")
    sgg3a = sb("sgg3", [128, 6, T], BF16)
    sgg3 = Tile(sgg3a, "sgg3")
    zbuf = [Tile(sb("zb%d" % i, [96, T + 3]), "zb") for i in range(2)]
    xio = [Tile(sb("xio%d" % i, [128, D]), "xio") for i in range(2)]
    wslots = [Tile(sb("wslot%d" % i, [128, SLOTW], BF16), "wslot") for i in range(4)]
    wrgT = [Tile(sb("wrgT%d" % i, [96, 1536], BF16), "wrgT") for i in range(2)]
    vt = [Tile(sb("vt%d" % s, [128, MIX], BF16), "vt") for s in range(4)]
    ktt = [Tile(sb("ktt%d" % i, [128, 384], BF16), "ktt") for i in range(2)]
    scT = [Tile(sb("scT%d" % i, [128, 512], BF16), "scT") for i in range(2)]
    onb = [Tile(sb("on%d" % i, [128, MIX], BF16), "on") for i in range(2)]
    ssq = [Tile(sb("ssq%d" % i, [128, 8]), "ssq") for i in range(2)]
    junk = Tile(sb("junk", [128, 192]), "junk")
    zd = Tile(sb("zd", [16, T], BF16), "zd")
    dcy = [Tile(sb("dcy%d" % hh, [96, 4]), "dcy") for hh in range(4)]

    NF, NBF = 15, 31
    fpool = [Tile(sb("fp%d" % i, [128, T]), "fp%d" % i) for i in range(NF)]
    bpool = [Tile(sb("bp%d" % i, [128, T], BF16), "bp%d" % i) for i in range(NBF)]
    ffree = list(fpool)
    bfree = list(bpool)

    def allocF():
        return ffree.pop(0)

    def freeF(t):
        ffree.append(t)

    def allocB():
        return bfree.pop(0)

    def freeB(t):
        bfree.append(t)

    banks = [Tile(nc.alloc_psum_tensor("psb%d" % i, [128, 512], F32), "ps%d" % i) for i in range(8)]
    bank_i = [0]

    def PS():
        b = banks[bank_i[0] % 8]
        bank_i[0] += 1
        return b

    def layer_slabs(l):
        o = [("winA0", win_d[l, 0, :, :], 128, 3072),
             ("winA1", win_d[l, 1, :, :], 128, 3072), ("wdec", wdec_d[l, :, :], 128, 128)]
        for i in range(2, 10):
            o.append(("win%d" % i, win_d[l, i, :, :], 128, 3072))
        for dc in range(8):
            o.append(("wg%d" % dc, win_d[l, 10 + dc, :, :], 128, 3072))
            o.append(("wb%d" % dc, wbr_d[l, dc, :, :], 128, 2816))
        for i in range(4):
            o.append(("wout%d" % i, wout_d[l, i, :, :], 128, 2048))
        for j in range(12):
            o.append(("wup%d" % j, wup_d[l, j, :, :], 128, 4096))
        for dc in range(8):
            o.append(("wdn%d" % dc, wdn_d[l, dc, :, :], 128, 3072))
        return [(k + "_%d" % l, ap, P, F) for (k, ap, P, F) in o]

    SLABS = []
    if STAGE >= 3:
        for l in range(L):
            for i in range(4):
                SLABS.append(("wmkv%d_%d" % (i, l), wmkv_d[l, i, :, :], 128, 3072))
    if STAGE >= 4:
        for it in range(NT):
            for l in range(L):
                SLABS.extend(layer_slabs(l))
    NSLOT, AHEAD = 4, 2
    slab_pos = [0, 0]

    def load_slab(key):
        idx = slab_pos[0]
        slab_pos[0] += 1
        assert SLABS[idx][0] == key, (SLABS[idx][0], key)
        while slab_pos[1] <= min(idx + AHEAD, len(SLABS) - 1):
            j = slab_pos[1]
            _, ap, P, F = SLABS[j]
            t = wslots[j % NSLOT]
            S.dma("pool", t.t[:P, :F], ap, W=[t])
            slab_pos[1] += 1
        return wslots[idx % NSLOT]

    def fsz(ap):
        n = 1
        for d in ap.shape[1:]:
            n *= d
        return n

    def mm(ps, out_ap, lhsT, rhs, start, stop, R, sig=None):
        d = max(0.065, fsz(out_ap) / 2400.0 + 0.012)
        if lhsT.dtype == F32:
            d *= 4.0
        S.op("pe", lambda e: e.matmul(out_ap, lhsT, rhs, start=start, stop=stop), R=R, W=[ps],
             sig=(stop if sig is None else sig), dur=d)

    def tr(ps, out_ap, in_ap, ident, R, sig=True):
        S.op("pe", lambda e: e.transpose(out_ap, in_ap, ident), R=R, W=[ps], sig=sig,
             dur=(0.3 if in_ap.dtype == F32 else 0.1))

    def act(out, in_, func, R, W, bias=0.0, scale=1.0, accum_out=None):
        d = 0.3 + fsz(out) / 1200.0
        if accum_out is None:
            S.op("act", lambda e: e.activation(out=out, in_=in_, func=func, bias=bias, scale=scale), R=R, W=W, dur=d)
        else:
            S.op("act", lambda e: e.activation(out=out, in_=in_, func=func, bias=bias, scale=scale,
                                               accum_out=accum_out), R=R, W=W, dur=d)

    def edur(E, out, f):
        n = fsz(out)
        if E == "pool":
            return 0.3 + n * 0.002
        return 0.12 + n * 0.00125 * f

    def tt(E, out, in0, in1, op, R, W):
        S.op(E, lambda e: e.tensor_tensor(out=out, in0=in0, in1=in1, op=op), R=R, W=W, dur=edur(E, out, 1.0))

    def ts(E, out, in0, s1, s2, op0, op1, R, W):
        if s2 is None:
            S.op(E, lambda e: e.tensor_scalar(out=out, in0=in0, scalar1=s1, scalar2=None, op0=op0), R=R, W=W,
                 dur=edur(E, out, 0.7))
        else:
            S.op(E, lambda e: e.tensor_scalar(out=out, in0=in0, scalar1=s1, scalar2=s2, op0=op0, op1=op1),
                 R=R, W=W, dur=edur(E, out, 0.7))

    def stt(E, out, in0, scalar, in1, op0, op1, R, W):
        S.op(E, lambda e: e.scalar_tensor_tensor(out=out, in0=in0, scalar=scalar, in1=in1, op0=op0, op1=op1),
             R=R, W=W, dur=edur(E, out, 1.0))

    def cp(E, out, in_, R, W):
        if E == "act":
            S.op("act", lambda e: e.activation(out=out, in_=in_, func=AF.Copy), R=R, W=W, dur=0.3 + fsz(out) / 1200.0)
        else:
            S.op(E, lambda e: e.tensor_copy(out=out, in_=in_), R=R, W=W, dur=edur(E, out, 0.7))

    def proj_fm(slot, ncol, col0, M, ps, rhs_tiles, n=T):
        for kc in range(8):
            mm(ps, ps.t[:M, :n], slot.t[:, kc * ncol + col0:kc * ncol + col0 + M], rhs_tiles[kc].t[:, :n],
               kc == 0, kc == 7, R=[slot, rhs_tiles[kc]])

    def rstd_of(Xc, n=T):
        ps = PS()
        for c in range(8):
            sq = allocB()
            eng = "dve" if c % 2 == 0 else "pool"
            tt(eng, sq.t[:, :n], Xc[c].t[:, :n], Xc[c].t[:, :n], ALU.mult, R=[Xc[c]], W=[sq])
            mm(ps, ps.t[:, :n], onesB, sq.t[:, :n], c == 0, c == 7, R=[cstB, sq], sig=True)
            freeB(sq)
        rstd = allocF()
        act(rstd.t[:, :n], ps.t[:, :n], AF.Ln, R=[ps, cstF], W=[rstd], scale=1.0 / D, bias=epsc)
        act(rstd.t[:, :n], rstd.t[:, :n], AF.Exp, R=[rstd], W=[rstd], scale=-0.5)
        return rstd

    def rms_to_bf(Xc, gcol, dst, n=T):
        xg = []
        for c in range(8):
            g = allocB()
            act(g.t[:, :n], Xc[c].t[:, :n], AF.Identity, R=[Xc[c], V128], W=[g],
                scale=V128.t[:, gcol + c:gcol + c + 1])
            xg.append(g)
        rstd = rstd_of(Xc, n)
        for c in range(8):
            eng = "pool" if c % 3 == 2 else "dve"
            tt(eng, dst[c].t[:, :n], xg[c].t[:, :n], rstd.t[:, :n], ALU.mult, R=[xg[c], rstd], W=[dst[c]])
            freeB(xg[c])
        freeF(rstd)

    S.dma("sp", cstF.t[:, :], cst_d[:, :], W=[cstF])
    cp("dve", cstB.t[:, 0:768], cstF.t[:, 0:768], R=[cstF], W=[cstB])

    def load_T(dst, dram, rows, width):
        r0 = 0
        while r0 < rows:
            n = min(128, rows - r0)
            st = allocF()
            S.dma("sp", st.t[:n, :width], dram[r0:r0 + n, :], W=[st])
            ps = PS()
            tr(ps, ps.t[:width, :n], st.t[:n, :width], identF[:n, :n], R=[st, cstF])
            cp("dve", dst.t[:width, r0:r0 + n], ps.t[:width, :n], R=[ps], W=[dst])
            freeF(st)
            r0 += n

    load_T(V128, p128_d, R128, 128)
    load_T(V96, p96_d, R96, 96)
    ts("dve", H128.t[:, :], V128.t[:, :], 0.5, None, ALU.mult, None, R=[V128], W=[H128])
    ts("dve", H96.t[:, :], V96.t[:, :], 0.5, None, ALU.mult, None, R=[V96], W=[H96])
    ts("dve", N96.t[:, :], V96.t[:, :], -1.0, None, ALU.mult, None, R=[V96], W=[N96])
    act(CL96.t[:, :], V96.t[:, :], AF.Exp, R=[V96], W=[CL96], scale=-1.0)
    act(CL96.t[:, :], CL96.t[:, :], AF.Ln, R=[CL96], W=[CL96], bias=1.0)
    ts("dve", HCL96.t[:, :], CL96.t[:, :], -4.0, None, ALU.mult, None, R=[CL96], W=[HCL96])
    ts("dve", CL96.t[:, :], CL96.t[:, :], -8.0, None, ALU.mult, None, R=[CL96], W=[CL96])
    ts("dve", CLM96.t[:, :], CL96.t[:, :], -2e-7, None, ALU.add, None, R=[CL96], W=[CLM96])
    for l in range(L if STAGE >= 2 else 0):
        S.dma("pool", wdcyB.t[:16, l * 384:(l + 1) * 384], wdcy_d[l, :, :], W=[wdcyB])

    S.op("pool", lambda e: e.memset(Sst_all[:, :, :], 0.0), W=[t for row in Sst for t in row])
    S.op("pool", lambda e: e.memset(rhalo_all[:, :, :, :], 0.0), W=[t for row in rhalo for t in row])
    S.op("pool", lambda e: e.memset(hst_all[:, :, :], 0.0), W=[t for row in hst for t in row])
    S.op("pool", lambda e: e.memset(fh_all[:, :, :, :], 0.0), W=[t for row in fh for t in row])

    memT = [allocF() for _ in range(8)]
    for mc in range(2):
        st = xio[mc]
        S.dma("sp", st.t[:, :], mem_d[mc * 128:(mc + 1) * 128, :], W=[st])
        for half in range(2):
            ps = PS()
            for j in range(4):
                c = half * 4 + j
                tr(ps, ps.t[:, j * 128:(j + 1) * 128], st.t[:, c * 128:(c + 1) * 128], identF, R=[st, cstF],
                   sig=(j == 3))
            for j in range(4):
                c = half * 4 + j
                cp("dve", memT[c].t[:, mc * 128:(mc + 1) * 128], ps.t[:, j * 128:(j + 1) * 128], R=[ps], W=[memT[c]])
    for l in range(L if STAGE >= 3 else 0):
        mb = [allocB() for _ in range(8)]
        rms_to_bf(memT, l * V128_L + O_NMEM, mb, n=MEM)
        KSUB = int(os.environ.get("KSUB", "9"))
        for i in range(4 if KSUB >= 3 else (2 if KSUB == 2 else 0)):
            slot = load_slab("wmkv%d_%d" % (i, l))
            if i < 2:
                for j in range(4):
                    cidx = i * 4 + j
                    ps = PS()
                    proj_fm(slot, 384, j * 96, 96, ps, mb, n=MEM)
                    act(MKT[l].t[:96, cidx * MEM:(cidx + 1) * MEM], ps.t[:96, :MEM], AF.Identity, R=[ps], W=[MKT[l]],
                        scale=float(192 ** -0.5))
            else:
                half = i - 2
                for mc in range(2):
                    ps = PS()
                    for kc in range(8):
                        mm(ps, ps.t[:, :384], mb[kc].t[:, mc * 128:(mc + 1) * 128],
                           slot.t[:, kc * 384:(kc + 1) * 384], kc == 0, kc == 7, R=[slot, mb[kc]])
                    cp("act", MV[l].t[:, mc * MIX + half * 384:mc * MIX + half * 384 + 384], ps.t[:, :384],
                       R=[ps], W=[MV[l]])
        for t in mb:
            freeB(t)
        if KSUB >= 4:
            S.dma("sp", mkt_scr[l].t[:, :], MKT_one.t[:, :], R=[MKT_one], W=[mkt_scr[l]])
            S.dma("sp", mv_scr[l].t[:, :], MV_one.t[:, :], R=[MV_one], W=[mv_scr[l]])
    for t in memT:
        freeF(t)

    mix_iter = [0]
    if STAGE >= 4:
        S.dma("pool", wrgT[0].t[:96, :], wrg_d[0, :, :], W=[wrgT[0]])

    def mixer(l):
        vb = l * V128_L
        cb = l * V96_L
        rms_to_bf(X, vb + O_NMIX, Hb)
        S.dma("sp", MKT_one.t[:, :], mkt_scr[l].t[:, :], R=[mkt_scr[l]], W=[MKT_one])
        S.dma("sp", MV_one.t[:, :], mv_scr[l].t[:, :], R=[mv_scr[l]], W=[MV_one])
        cp("act", Sb_one.t[:96, :], Sst_all[:, l, :], R=Sst[l], W=[Sb_one])

        wrg = wrgT[mix_iter[0] % 2]
        outA = [None] * 8
        slotA = None
        xrs, xrbs = [], []
        for h in range(8):
            if h % 4 == 0:
                slotA = load_slab("winA%d_%d" % (h // 4, l))
            ps = PS()
            proj_fm(slotA, 384, (h % 4) * 96, 96, ps, Hb)
            zb = zbuf[h % 2]
            cp("pool", zb.t[:, 0:3], rhalo[l][h].t, R=[rhalo[l][h]], W=[zb])
            cp("act", zb.t[:, 3:3 + T], ps.t[:96, :T], R=[ps], W=[zb])
            cp("pool", rhalo[l][h].t, zb.t[:, T:T + 3], R=[zb], W=[rhalo[l][h]])
            xr = allocF()

            def cw(k):
                return V96.t[:, cb + O_CW0 + 8 * k + h:cb + O_CW0 + 8 * k + h + 1]
            ts("dve", xr.t[:96, :], zb.t[:, 3:3 + T], cw(3), V96.t[:, cb + O_CB + h:cb + O_CB + h + 1],
               ALU.mult, ALU.add, R=[zb, V96], W=[xr])
            for k in (2, 1, 0):
                stt("dve", xr.t[:96, :], zb.t[:, k:k + T], cw(k), xr.t[:96, :], ALU.mult, ALU.add,
                    R=[zb, V96, xr], W=[xr])
            xrb = allocB()
            cp("act", xrb.t[:96, :], xr.t[:96, :], R=[xr], W=[xrb])
            xrs.append(xr)
            xrbs.append(xrb)
        def rg2(h):
            xr, xrb = xrs[h], xrbs[h]
            psa = PS()
            mm(psa, psa.t[:96, :T], wrg.t[:96, h * 96:(h + 1) * 96], xrb.t[:96, :T], True, True, R=[wrg, xrb])
            psx = PS()
            mm(psx, psx.t[:96, :T], wrg.t[:96, (8 + h) * 96:(9 + h) * 96], xrb.t[:96, :T], True, True, R=[wrg, xrb])
            freeB(xrb)
            ta = allocF()
            ti = allocF()
            act(ta.t[:96, :], psa.t[:96, :T], AF.Tanh, R=[psa, H96], W=[ta],
                bias=H96.t[:, cb + O_BA + h:cb + O_BA + h + 1], scale=0.5)
            act(ti.t[:96, :], psx.t[:96, :T], AF.Tanh, R=[psx, H96], W=[ti],
                bias=H96.t[:, cb + O_BX + h:cb + O_BX + h + 1], scale=0.5)
            aa = allocF()
            e2 = allocF()
            hcl = HCL96.t[:, cb + O_LAM + h:cb + O_LAM + h + 1]
            cl = CL96.t[:, cb + O_LAM + h:cb + O_LAM + h + 1]
            act(aa.t[:96, :], ta.t[:96, :], AF.Exp, R=[ta, HCL96], W=[aa], bias=hcl, scale=hcl)
            act(e2.t[:96, :], ta.t[:96, :], AF.Exp, R=[ta, CL96, CLM96], W=[e2],
                bias=CLM96.t[:, cb + O_LAM + h:cb + O_LAM + h + 1], scale=cl)
            stt("dve", ti.t[:96, :], ti.t[:96, :], 1.0, xr.t[:96, :], ALU.add, ALU.mult, R=[ti, xr], W=[ti])
            act(e2.t[:96, :], e2.t[:96, :], AF.Ln, R=[e2], W=[e2], scale=-0.25, bias=0.25)
            act(e2.t[:96, :], e2.t[:96, :], AF.Exp, R=[e2], W=[e2], scale=0.5)
            tt("dve", e2.t[:96, :], e2.t[:96, :], ti.t[:96, :], ALU.mult, R=[e2, ti], W=[e2])
            S.op("dve", lambda e: e.tensor_tensor_scan(out=ta.t[:96, :], data0=aa.t[:96, :], data1=e2.t[:96, :],
                                                       initial=hst[l][h].t, op0=ALU.mult, op1=ALU.add),
                 R=[aa, e2, hst[l][h]], W=[ta])
            oa = allocB()
            cp("act", oa.t[:96, :], ta.t[:96, :], R=[ta], W=[oa])
            cp("pool", hst[l][h].t, ta.t[:96, T - 1:T], R=[ta], W=[hst[l][h]])
            outA[h] = oa
            for t in (xr, ta, ti, aa, e2):
                freeF(t)

        mix_iter[0] += 1
        if mix_iter[0] < NT * L:
            S.dma("pool", wrgT[mix_iter[0] % 2].t[:96, :], wrg_d[mix_iter[0] % L, :, :], W=[wrgT[mix_iter[0] % 2]])

        wdec = load_slab("wdec_%d" % l)
        ps = PS()
        proj_fm(wdec, 16, 0, 16, ps, Hb)
        cp("act", zd.t[:16, :], ps.t[:16, :T], R=[ps], W=[zd])
        Epos, Eneg = [], []
        for hh in range(4):
            ps = PS()
            mm(ps, ps.t[:96, :T], wdcyB.t[:16, l * 384 + hh * 96:l * 384 + (hh + 1) * 96], zd.t[:16, :T], True, True,
               R=[wdcyB, zd])
            e = allocF()
            act(e.t[:96, :], ps.t[:96, :T], AF.Exp, R=[ps, N96], W=[e],
                bias=N96.t[:, cb + O_BD + hh:cb + O_BD + hh + 1], scale=-1.0)
            act(e.t[:96, :], e.t[:96, :], AF.Ln, R=[e], W=[e], bias=1.0)
            cs = allocF()
            S.op("dve", lambda en, cs=cs, e=e: en.tensor_tensor_scan(out=cs.t[:96, :], data0=smask[:96, :],
                                                                     data1=e.t[:96, :], initial=0.0,
                                                                     op0=ALU.mult, op1=ALU.add),
                 R=[cstF, e], W=[cs], dur=0.8)
            ep = allocB()
            en_ = allocB()
            act(ep.t[:96, :], cs.t[:96, :], AF.Exp, R=[cs], W=[ep], scale=-1.0 / 16.0,
                bias=float(np.log(96.0 ** -0.5)))
            act(en_.t[:96, :], cs.t[:96, :], AF.Exp, R=[cs], W=[en_], scale=1.0 / 16.0)
            act(dcy[hh].t[:96, 0:4], cs.t[:96, C_GLA - 1::C_GLA], AF.Exp, R=[cs], W=[dcy[hh]], scale=-1.0 / 16.0)
            Epos.append(ep)
            Eneg.append(en_)
            freeF(e)
            freeF(cs)
        slot = load_slab("win2_%d" % l)
        qe = []
        for hh in range(4):
            ps = PS()
            proj_fm(slot, 384, hh * 96, 96, ps, Hb)
            q = allocB()
            tt("dve", q.t[:96, :], ps.t[:96, :T], Epos[hh].t[:96, :], ALU.mult, R=[ps, Epos[hh]], W=[q])
            qe.append(q)
            freeB(Epos[hh])
        slot = load_slab("win3_%d" % l)
        ke = []
        for hh in range(4):
            ps = PS()
            proj_fm(slot, 384, hh * 96, 96, ps, Hb)
            k = allocB()
            tt("dve", k.t[:96, :], ps.t[:96, :T], Eneg[hh].t[:96, :], ALU.mult, R=[ps, Eneg[hh]], W=[k])
            ke.append(k)
            freeB(Eneg[hh])
        for half in range(2):
            slot = load_slab("win%d_%d" % (4 + half, l))
            for s in range(4):
                ps = PS()
                for kc in range(8):
                    mm(ps, ps.t[:, :384], Hb[kc].t[:, s * 128:(s + 1) * 128], slot.t[:, kc * 384:(kc + 1) * 384],
                       kc == 0, kc == 7, R=[slot, Hb[kc]])
                cp("act", vt[s].t[:, half * 384:(half + 1) * 384], ps.t[:, :384], R=[ps], W=[vt[s]])
        for half in range(2):
            slot = load_slab("win%d_%d" % (6 + half, l))
            for j in range(3):
                c = half * 3 + j
                ps = PS()
                proj_fm(slot, 384, j * 128, 128, ps, Hb)
                tg = allocF()
                act(tg.t[:, :], ps.t[:, :T], AF.Tanh, R=[ps], W=[tg], scale=0.5)
                stt("dve", tg.t[:, :], tg.t[:, :], 1.0, ps.t[:, :T], ALU.add, ALU.mult, R=[tg, ps], W=[tg])
                act(sgg3a[:, c, :], tg.t[:, :], AF.Identity, R=[tg, H128], W=[sgg3],
                    scale=H128.t[:, vb + O_GLAN + c:vb + O_GLAN + c + 1])
                freeF(tg)

        kts = [None] * 4
        scs = [None] * 4
        pskvs = [None] * 4

        def gla_A(s):
            tsl = slice(s * 128, (s + 1) * 128)
            pst = PS()
            pstb = pst.t[:, :].bitcast(BF16)
            for hh in range(4):
                tr(pst, pstb[:, hh * 96:(hh + 1) * 96], ke[hh].t[:96, tsl], identB[:96, :96], R=[ke[hh], cstB],
                   sig=(hh == 3))
            kt = ktt[s % 2]
            cp("act", kt.t[:, :], pstb[:, 0:384], R=[pst], W=[kt])
            pss = PS()
            for hh in range(4):
                mm(pss, pss.t[:, hh * 128:(hh + 1) * 128], ke[hh].t[:96, tsl], qe[hh].t[:96, tsl], True, True,
                   R=[ke[hh], qe[hh]])
            sc = scT[s % 2]
            tt("dve", sc.t[:, :], pss.t[:, :], tri4B, ALU.mult, R=[pss, cstB], W=[sc])
            kts[s] = kt
            scs[s] = sc

        def gla_KV(s):
            kt = kts[s]
            pk = []
            for hp in range(2):
                pskv = PS()
                for hh in (2 * hp, 2 * hp + 1):
                    oc = (hh % 2) * 192
                    mm(pskv, pskv.t[:96, oc:oc + 192], kt.t[:, hh * 96:(hh + 1) * 96],
                       vt[s].t[:, hh * 192:(hh + 1) * 192], True, True, R=[kt, vt[s]])
                pk.append(pskv)
            pskvs[s] = pk

        ons = [None] * 4

        def gla_B(s):
            tsl = slice(s * 128, (s + 1) * 128)
            sc = scs[s]
            on = onb[s % 2]
            sq = ssq[s % 2]
            S.op("pool", lambda e: e.memset(sq.t[:, 0:4], 0.0), W=[sq])
            psos = []
            for hp in range(2):
                pso = PS()
                psos.append(pso)
                for hh in (2 * hp, 2 * hp + 1):
                    oc = (hh % 2) * 192
                    mm(pso, pso.t[:, oc:oc + 192], sc.t[:, hh * 128:(hh + 1) * 128], vt[s].t[:, hh * 192:(hh + 1) * 192],
                       True, False, R=[sc, vt[s]])
                    mm(pso, pso.t[:, oc:oc + 192], qe[hh].t[:96, tsl], Sb_t[l].t[:96, hh * 192:(hh + 1) * 192],
                       False, True, R=[qe[hh], Sb_t[l]])
                for hh in (2 * hp, 2 * hp + 1):
                    oc = (hh % 2) * 192
                    act(junk.t[:, :], pso.t[:, oc:oc + 192], AF.Square, R=[pso], W=[junk, sq],
                        accum_out=sq.t[:, hh:hh + 1])
            act(sq.t[:, 4:8], sq.t[:, 0:4], AF.Ln, R=[sq, cstF], W=[sq], scale=1.0 / 192.0, bias=epsc)
            act(sq.t[:, 4:8], sq.t[:, 4:8], AF.Exp, R=[sq], W=[sq], scale=-0.5)
            for hh in range(4):
                pso = psos[hh // 2]
                oc = (hh % 2) * 192
                ts("dve", on.t[:, hh * 192:(hh + 1) * 192], pso.t[:, oc:oc + 192], sq.t[:, 4 + hh:5 + hh], None,
                   ALU.mult, None, R=[pso, sq], W=[on])
            ons[s] = on

        def gla_U(s):
            for hp in range(2):
                pskv = pskvs[s][hp]
                for hh in (2 * hp, 2 * hp + 1):
                    oc = (hh % 2) * 192
                    st_ = Sst[l][hh]
                    ts("dve", st_.t, st_.t, dcy[hh].t[:96, s:s + 1], None, ALU.mult, None, R=[st_, dcy[hh]], W=[st_])
                    stt("dve", st_.t, pskv.t[:96, oc:oc + 192], dcy[hh].t[:96, s:s + 1], st_.t, ALU.mult, ALU.add,
                        R=[pskv, dcy[hh], st_], W=[st_])
            cp("act", Sb_t[l].t[:96, :], Sst_all[:, l, :], R=Sst[l], W=[Sb_t[l]])

        def gla_C(s):
            tsl = slice(s * 128, (s + 1) * 128)
            on = ons[s]
            pst2 = PS()
            pst2b = pst2.t[:, :].bitcast(BF16)
            for c in range(6):
                tr(pst2, pst2b[:, c * 128:(c + 1) * 128], on.t[:, c * 128:(c + 1) * 128], identB, R=[on, cstB],
                   sig=(c == 5))
            tt("dve", outB3a[:, :, tsl], pst2b[:, 0:768].rearrange("p (c i) -> p c i", c=6), sgg3a[:, :, tsl],
               ALU.mult, R=[pst2, sgg3], W=[outB3])

        mqT = [None] * 8
        mq_slot = [None]

        def mem_Q(c):
            if c % 4 == 0:
                mq_slot[0] = load_slab("win%d_%d" % (8 + c // 4, l))
            ps = PS()
            proj_fm(mq_slot[0], 384, (c % 4) * 96, 96, ps, Hb)
            m = allocB()
            cp("act", m.t[:96, :], ps.t[:96, :T], R=[ps], W=[m])
            mqT[c] = m

        pTs = [None] * 4
        outC = [None] * 8

        def mem_1(h):
            pT = [allocB(), allocB()]
            for mc in range(2):
                ps = PS()
                for j in range(2):
                    cidx = 2 * h + j
                    mm(ps, ps.t[:, :T], MKT[l].t[:96, cidx * MEM + mc * 128:cidx * MEM + (mc + 1) * 128],
                       mqT[cidx].t[:96, :T], j == 0, j == 1, R=[MKT[l], mqT[cidx]])
                act(pT[mc].t[:, :], ps.t[:, :T], AF.Exp, R=[ps], W=[pT[mc]])
            pTs[h] = pT

        def mem_2(h):
            pT = pTs[h]
            psr = PS()
            for mc in range(2):
                mm(psr, psr.t[:96, :T], onesB[:, :96], pT[mc].t[:, :T], mc == 0, mc == 1, R=[cstB, pT[mc]])
            rinv = allocF()
            act(rinv.t[:96, :], psr.t[:96, :T], AF.Ln, R=[psr], W=[rinv])
            act(rinv.t[:96, :], rinv.t[:96, :], AF.Exp, R=[rinv], W=[rinv], scale=-1.0)
            for j in range(2):
                pso = PS()
                for mc in range(2):
                    c0 = mc * MIX + h * 192 + j * 96
                    mm(pso, pso.t[:96, :T], MV[l].t[:, c0:c0 + 96], pT[mc].t[:, :T], mc == 0, mc == 1,
                       R=[MV[l], pT[mc]])
                oc_ = allocB()
                tt("dve", oc_.t[:96, :], pso.t[:96, :T], rinv.t[:96, :], ALU.mult, R=[pso, rinv], W=[oc_])
                outC[2 * h + j] = oc_
            freeF(rinv)
            for t in pT:
                freeB(t)
            freeB(mqT[2 * h])
            freeB(mqT[2 * h + 1])

        gla_A(0)
        gla_A(1)
        for c in range(0, 4):
            mem_Q(c)
        rg2(0)
        gla_B(0)
        gla_KV(0)
        gla_U(0)
        for c in range(4, 8):
            mem_Q(c)
        rg2(1)
        gla_A(2)
        gla_B(1)
        gla_KV(1)
        gla_U(1)
        mem_1(0)
        rg2(2)
        gla_C(0)
        gla_A(3)
        mem_1(1)
        rg2(3)
        gla_B(2)
        gla_KV(2)
        gla_U(2)
        mem_2(0)
        rg2(4)
        gla_C(1)
        mem_1(2)
        gla_B(3)
        gla_KV(3)
        gla_U(3)
        rg2(5)
        mem_2(1)
        gla_C(2)
        mem_1(3)
        rg2(6)
        mem_2(2)
        gla_C(3)
        rg2(7)
        mem_2(3)
        for t in qe + ke:
            freeB(t)

        merged = []
        for dc in range(8):
            wg = load_slab("wg%d_%d" % (dc, l))
            wb = load_slab("wb%d_%d" % (dc, l))
            acc = None
            for b in range(3):
                psg = PS()
                proj_fm(wg, 384, b * 128, 128, psg, Hb)
                tg = allocF()
                act(tg.t[:, :], psg.t[:, :T], AF.Tanh, R=[psg], W=[tg], scale=0.5)
                psp = PS()
                if b == 0:
                    ch = [(96, wb.t[:96, i * 128:(i + 1) * 128], outA[i].t[:96, :T], outA[i]) for i in range(8)]
                elif b == 1:
                    ch = [(128, wb.t[:, (8 + i) * 128:(9 + i) * 128], outB3a[:, i, :], outB3) for i in range(6)]
                else:
                    ch = [(96, wb.t[:96, (14 + i) * 128:(15 + i) * 128], outC[i].t[:96, :T], outC[i]) for i in range(8)]
                for i, (K, lh, rh, rt) in enumerate(ch):
                    mm(psp, psp.t[:, :T], lh, rh, i == 0, i == len(ch) - 1, R=[wb, rt])
                stt("dve", tg.t[:, :], tg.t[:, :], 1.0, psp.t[:, :T], ALU.add, ALU.mult, R=[tg, psp], W=[tg])
                if acc is None:
                    acc = tg
                else:
                    tt("pool", acc.t[:, :], acc.t[:, :], tg.t[:, :], ALU.add, R=[acc, tg], W=[acc])
                    freeF(tg)
            mg = allocB()
            cp("act", mg.t[:, :], acc.t[:, :], R=[acc], W=[mg])
            freeF(acc)
            merged.append(mg)
        for t in outA + outC:
            freeB(t)

        slot = None
        for dc in range(8):
            if dc % 2 == 0:
                slot = load_slab("wout%d_%d" % (dc // 2, l))
            ps = PS()
            proj_fm(slot, 256, (dc % 2) * 128, 128, ps, merged)
            stt("dve", X[dc].t, ps.t[:, :T], 0.5, X[dc].t, ALU.mult, ALU.add, R=[ps, X[dc]], W=[X[dc]])
        for t in merged:
            freeB(t)

    def ffn(l):
        vb = l * V128_L
        rms_to_bf(X, vb + O_NFFN, Hb)
        Pm = []
        for j in range(12):
            slot = load_slab("wup%d_%d" % (j, l))
            for pi in range(2):
                g = 2 * j + pi
                accs = []
                for vi in range(2):
                    chn = g + 24 * vi
                    ps = PS()
                    proj_fm(slot, 512, pi * 256 + vi * 128, 128, ps, Hb)
                    acc = allocF()
                    w0 = V128.t[:, vb + O_FW0 + chn:vb + O_FW0 + chn + 1]
                    w1 = V128.t[:, vb + O_FW1 + chn:vb + O_FW1 + chn + 1]
                    w2 = V128.t[:, vb + O_FW2 + chn:vb + O_FW2 + chn + 1]
                    bb = V128.t[:, vb + O_FB + chn:vb + O_FB + chn + 1]
                    act(acc.t[:, :], ps.t[:, :T], AF.Identity, R=[ps, V128], W=[acc], bias=bb, scale=w2)
                    stt("dve", acc.t[:, 1:T], ps.t[:, 0:T - 1], w1, acc.t[:, 1:T], ALU.mult, ALU.add,
                        R=[ps, V128, acc], W=[acc])
                    stt("dve", acc.t[:, 2:T], ps.t[:, 0:T - 2], w0, acc.t[:, 2:T], ALU.mult, ALU.add,
                        R=[ps, V128, acc], W=[acc])
                    fht = fh[l][chn]
                    stt("dve", acc.t[:, 0:2], fht.t[:, 0:2], w0, acc.t[:, 0:2], ALU.mult, ALU.add,
                        R=[fht, V128, acc], W=[acc])
                    stt("dve", acc.t[:, 0:1], fht.t[:, 1:2], w1, acc.t[:, 0:1], ALU.mult, ALU.add,
                        R=[fht, V128, acc], W=[acc])
                    cp("act", fht.t[:, 0:2], ps.t[:, T - 2:T], R=[ps], W=[fht])
                    accs.append(acc)
                act(accs[0].t[:, :], accs[0].t[:, :], AF.Gelu_apprx_tanh, R=[accs[0]], W=[accs[0]])
                pm = allocB()
                tt("pool", pm.t[:, :], accs[0].t[:, :], accs[1].t[:, :], ALU.mult, R=accs, W=[pm])
                Pm.append(pm)
                freeF(accs[0])
                freeF(accs[1])
        for dc in range(8):
            slot = load_slab("wdn%d_%d" % (dc, l))
            ps = PS()
            for kc in range(24):
                mm(ps, ps.t[:, :T], slot.t[:, kc * 128:(kc + 1) * 128], Pm[kc].t[:, :T], kc == 0, kc == 23,
                   R=[slot, Pm[kc]])
            tt("dve", X[dc].t, ps.t[:, :T], X[dc].t, ALU.add, R=[ps, X[dc]], W=[X[dc]])
        for t in Pm:
            freeB(t)

    out_tiles = []
    for it in range(NT):
        t0 = it * T
        for s in range(4):
            st = xio[s % 2]
            S.dma("sp", st.t[:, :], x_d[t0 + s * 128:t0 + (s + 1) * 128, :], W=[st])
            for half in range(2):
                ps = PS()
                for j in range(4):
                    c = half * 4 + j
                    tr(ps, ps.t[:, j * 128:(j + 1) * 128], st.t[:, c * 128:(c + 1) * 128], identF, R=[st, cstF],
                       sig=(j == 3))
                cp("act", Xall[:, half * 4:half * 4 + 4, s * 128:(s + 1) * 128],
                   ps.t[:, :].rearrange("p (c i) -> p c i", c=4), R=[ps], W=X[half * 4:half * 4 + 4])
        for l in range(L if STAGE >= 4 else 0):
            if STAGE != 6:
                mixer(l)
            if STAGE != 5:
                ffn(l)
        rstd = rstd_of(X)
        for c in range(8):
            stt("dve", X[c].t, X[c].t, V128.t[:, O_FINAL + c:O_FINAL + c + 1], rstd.t[:, :], ALU.mult, ALU.mult,
                R=[X[c], V128, rstd], W=[X[c]])
        freeF(rstd)
        for s in range(4):
            st = xio[s % 2]
            for half in range(2):
                ps = PS()
                for j in range(4):
                    c = half * 4 + j
                    tr(ps, ps.t[:, j * 128:(j + 1) * 128], Xall[:, c, s * 128:(s + 1) * 128], identF, R=[X[c], cstF],
                       sig=(j == 3))
                cp("act", st.t[:, half * 512:(half + 1) * 512], ps.t[:, :], R=[ps], W=[st])
            S.dma("sp", y_d[t0 + s * 128:t0 + (s + 1) * 128, :], st.t[:, :], R=[st])
            out_tiles.append(st)

    S.schedule_and_emit(reorder=(os.environ.get("KNOREORDER") is None))
    for st in xio:
        if st.dsem is not None and st.dcnt:
            nc.sync.wait_ge(st.dsem, st.dcnt)
    build.stats = (S.ninst, S.nwait, nc.sbuf_bytes_remaining, S.sim_time)
    return nc


_CACHE = {}


def prep_inputs(inp, L):
    wts = pack_weights(inp, L)
    p128, p96 = pack_small(inp, L)
    cst = make_consts()
    return wts, p128, p96, cst


def kernel(**inputs):
    inp = {k: np.asarray(v) for k, v in inputs.items()}
    x = inp["x"]
    B, S_TOK, _ = x.shape
    L = inp["w_in"].shape[0]
    wts, p128, p96, cst = prep_inputs(inp, L)
    nc = build(L, S_TOK)
    in_maps = []
    for b in range(B):
        m = {"x": np.ascontiguousarray(x[b]), "mem": np.ascontiguousarray(inp["mem"][b]),
             "p128": p128, "p96": p96, "cst": cst}
        m.update(wts)
        in_maps.append(m)
    res = run_bass_kernel_spmd(nc, in_maps, core_ids=list(range(B)))
    out = np.stack([np.asarray(r["y"]) for r in res.results], axis=0)
    return out.astype(np.float32)
```
